# Optimizing a Trainium2 kernel written in Bass

```python
import jax, jax.numpy as jnp
from jax import lax
import numpy as np

D_MODEL = 1024
BATCH = 8
SEQ = 4096
DEPTH = 2

D_MIX = D_MODEL
D_SGU = D_MIX // 2
SGU_GROUPS = 4
SGU_GROUP_DIM = D_SGU // SGU_GROUPS
CHUNK = 128
D_ATTN = D_MIX - D_SGU
N_HEADS = 4
HEAD_DIM = D_ATTN // N_HEADS // 2
V_HEAD_DIM = 2 * HEAD_DIM
ROT_DIM = HEAD_DIM // 4
ROPE_THETA = 500000.0
Q_BLOCK = 128
D_FF = 2816
CONV_WIDTH = 3
EPS = 1e-6
D_IN = 2 * D_SGU + 3 * D_ATTN

kernel_name = "hybrid_sgu_diffattn_convffn"


def rmsnorm(x, g):
    xf = x.astype(jnp.float32)
    y = xf * lax.rsqrt(jnp.mean(xf * xf, axis=-1, keepdims=True) + EPS)
    return (y * g.astype(jnp.float32)).astype(x.dtype)


def apply_partial_rope(t, cos, sin):
    half = ROT_DIM // 2
    t1 = t[..., :half]
    t2 = t[..., half:ROT_DIM]
    rot = jnp.concatenate([t1 * cos - t2 * sin, t2 * cos + t1 * sin], axis=-1)
    return jnp.concatenate([rot.astype(t.dtype), t[..., ROT_DIM:]], axis=-1)


def sgu_mixer(u, v, v_gain, w_s, b_s):
    B, S, _ = u.shape
    n_chunks = S // CHUNK
    vg = rmsnorm(v.reshape(B, S, SGU_GROUPS, SGU_GROUP_DIM), v_gain.reshape(SGU_GROUPS, SGU_GROUP_DIM))
    vg = vg.reshape(B, n_chunks, CHUNK, SGU_GROUPS, SGU_GROUP_DIM)
    causal = jnp.tril(jnp.ones((CHUNK, CHUNK), dtype=bool))
    w = jnp.where(causal[None], w_s, jnp.zeros_like(w_s))
    mixed = jnp.einsum('gpq,bcqgd->bcpgd', w, vg) + b_s.T[None, None, :, :, None]
    return u * mixed.reshape(B, S, D_SGU)


def diff_attention(q, k, v, positions, lq1, lk1, lq2, lk2, subln_g, layer_idx):
    B, S, _ = q.shape
    q = q.reshape(B, S, N_HEADS, 2, HEAD_DIM)
    k = k.reshape(B, S, N_HEADS, 2, HEAD_DIM)
    v = v.reshape(B, S, N_HEADS, V_HEAD_DIM)
    inv_freq = ROPE_THETA ** (-jnp.arange(0, ROT_DIM, 2, dtype=jnp.float32) / ROT_DIM)
    ang = positions.astype(jnp.float32)[..., None] * inv_freq
    cos = jnp.cos(ang)[:, :, None, None, :]
    sin = jnp.sin(ang)[:, :, None, None, :]
    q = apply_partial_rope(q, cos, sin)
    k = apply_partial_rope(k, cos, sin)

    lambda_init = 0.8 - 0.6 * float(np.exp(-0.3 * layer_idx))
    lam = (jnp.exp(jnp.sum(lq1.astype(jnp.float32) * lk1.astype(jnp.float32)))
           - jnp.exp(jnp.sum(lq2.astype(jnp.float32) * lk2.astype(jnp.float32)))
           + lambda_init)
    scale = HEAD_DIM ** -0.5

    outs = []
    for start in range(0, S, Q_BLOCK):
        end = start + Q_BLOCK
        qb = q[:, start:end]
        kb = k[:, :end]
        scores = jnp.einsum('bqhmd,bkhmd->bhmqk', qb, kb).astype(jnp.float32) * scale
        q_idx = start + jnp.arange(Q_BLOCK)
        k_idx = jnp.arange(end)
        mask = k_idx[None, :] <= q_idx[:, None]
        scores = jnp.where(mask, scores, -jnp.inf)
        p = jax.nn.softmax(scores, axis=-1)
        attn = p[:, :, 0] - lam * p[:, :, 1]
        outs.append(jnp.einsum('bhqk,bkhd->bqhd', attn.astype(v.dtype), v[:, :end]))
    o = jnp.concatenate(outs, axis=1)
    o = rmsnorm(o, subln_g) * (1.0 - lambda_init)
    return o.reshape(B, S, D_ATTN)


def conv_ffn(h, w_up, conv_w, conv_b, w_down):
    S = h.shape[1]
    up = h @ w_up
    up_p = jnp.pad(up, ((0, 0), (CONV_WIDTH - 1, 0), (0, 0)))
    conv = conv_b
    for j in range(CONV_WIDTH):
        conv = conv + up_p[:, j:j + S] * conv_w[j]
    gate, val = jnp.split(conv, 2, axis=-1)
    return (jax.nn.silu(gate) * val) @ w_down


def setup_inputs(seed: int = 0) -> dict:
    key = jax.random.key(seed)
    ks = jax.random.split(key, 20)
    f32 = jnp.float32
    x = jax.random.normal(ks[0], (BATCH, SEQ, D_MODEL), f32)
    positions = jnp.tile(jnp.arange(SEQ, dtype=jnp.int32)[None, :], (BATCH, 1))
    attn_norm = 1.0 + 0.02 * jax.random.normal(ks[1], (DEPTH, D_MODEL), f32)
    w_in = jax.random.normal(ks[2], (DEPTH, D_MODEL, D_IN), f32) * D_MODEL ** -0.5
    sgu_v_norm = 1.0 + 0.02 * jax.random.normal(ks[3], (DEPTH, D_SGU), f32)
    sgu_w_spatial = jax.random.normal(ks[4], (DEPTH, SGU_GROUPS, CHUNK, CHUNK), f32) * CHUNK ** -0.5
    sgu_b_spatial = 1.0 + 0.02 * jax.random.normal(ks[5], (DEPTH, SGU_GROUPS, CHUNK), f32)
    lambda_q1 = 0.1 * jax.random.normal(ks[6], (DEPTH, HEAD_DIM), f32)
    lambda_k1 = 0.1 * jax.random.normal(ks[7], (DEPTH, HEAD_DIM), f32)
    lambda_q2 = 0.1 * jax.random.normal(ks[8], (DEPTH, HEAD_DIM), f32)
    lambda_k2 = 0.1 * jax.random.normal(ks[9], (DEPTH, HEAD_DIM), f32)
    subln_gain = 1.0 + 0.02 * jax.random.normal(ks[10], (DEPTH, V_HEAD_DIM), f32)
    w_out = jax.random.normal(ks[11], (DEPTH, D_MIX, D_MODEL), f32) * D_MIX ** -0.5
    ffn_norm = 1.0 + 0.02 * jax.random.normal(ks[12], (DEPTH, D_MODEL), f32)
    w_up = jax.random.normal(ks[13], (DEPTH, D_MODEL, 2 * D_FF), f32) * D_MODEL ** -0.5
    conv_w = jax.random.normal(ks[14], (DEPTH, CONV_WIDTH, 2 * D_FF), f32) * CONV_WIDTH ** -0.5
    conv_b = 0.02 * jax.random.normal(ks[15], (DEPTH, 2 * D_FF), f32)
    w_down = jax.random.normal(ks[16], (DEPTH, D_FF, D_MODEL), f32) * D_FF ** -0.5
    final_norm = 1.0 + 0.02 * jax.random.normal(ks[17], (D_MODEL,), f32)
    return {"x": x, "positions": positions, "attn_norm": attn_norm, "w_in": w_in,
            "sgu_v_norm": sgu_v_norm, "sgu_w_spatial": sgu_w_spatial, "sgu_b_spatial": sgu_b_spatial,
            "lambda_q1": lambda_q1, "lambda_k1": lambda_k1, "lambda_q2": lambda_q2, "lambda_k2": lambda_k2,
            "subln_gain": subln_gain, "w_out": w_out, "ffn_norm": ffn_norm, "w_up": w_up,
            "conv_w": conv_w, "conv_b": conv_b, "w_down": w_down, "final_norm": final_norm}


def reference(x, positions, attn_norm, w_in, sgu_v_norm, sgu_w_spatial, sgu_b_spatial,
              lambda_q1, lambda_k1, lambda_q2, lambda_k2, subln_gain, w_out, ffn_norm,
              w_up, conv_w, conv_b, w_down, final_norm):
    h = x
    for l in range(DEPTH):
        n = rmsnorm(h, attn_norm[l])
        proj = n @ w_in[l]
        u, v_sgu, q, k, v_att = jnp.split(
            proj, [D_SGU, 2 * D_SGU, 2 * D_SGU + D_ATTN, 2 * D_SGU + 2 * D_ATTN], axis=-1)
        u = jax.nn.gelu(u)
        v_sgu = jax.nn.gelu(v_sgu)
        y_sgu = sgu_mixer(u, v_sgu, sgu_v_norm[l], sgu_w_spatial[l], sgu_b_spatial[l])
        y_att = diff_attention(q, k, v_att, positions, lambda_q1[l], lambda_k1[l],
                               lambda_q2[l], lambda_k2[l], subln_gain[l], l)
        mixed = jnp.concatenate([y_sgu, y_att], axis=-1)
        h = h + mixed @ w_out[l]
        h = h + conv_ffn(rmsnorm(h, ffn_norm[l]), w_up[l], conv_w[l], conv_b[l], w_down[l])
    return rmsnorm(h, final_norm)
```

```python
import contextlib
import math
import numpy as np
import concourse.bass as bass
import concourse.mybir as mybir
from concourse.bass_utils import run_bass_kernel_spmd

F32 = mybir.dt.float32
BF16 = mybir.dt.bfloat16
I32 = mybir.dt.int32
AF = mybir.ActivationFunctionType
ALU = mybir.AluOpType
AX = mybir.AxisListType

D = 1024
DIN = 2560
DFF = 2816
NCG = DFF // 128
NH = 4
EPS = 1e-6
DEPTH = 2
ROPE_THETA = 500000.0
VW = 130


class Tik:
    __slots__ = ("sem", "val")

    def __init__(self, sem, val):
        self.sem = sem
        self.val = val


class Dep:
    __slots__ = ("name", "w", "r", "dsem", "dcnt")

    def __init__(self, name):
        self.name = name
        self.w = None
        self.r = {}
        self.dsem = None
        self.dcnt = 0


class Eng:
    def __init__(self, kb, name, eng, is_pe=False):
        self.kb = kb
        self.name = name
        self.eng = eng
        self.is_pe = is_pe
        self.sem = kb.new_sem("e_" + name)
        self.cnt = 0
        self.waited = {}
        self.pending = Tik(self.sem, None)


class KB:
    def __init__(self, nc):
        self.nc = nc
        self.stack = contextlib.ExitStack()
        self.nsem = 0
        self.pe = Eng(self, "pe", nc.tensor, True)
        self.act = Eng(self, "act", nc.scalar)
        self.dve = Eng(self, "dve", nc.vector)
        self.pool = Eng(self, "pool", nc.gpsimd)
        self.sp = Eng(self, "sp", nc.sync)
        self.nins = 0
        self.owners = []

    def new_sem(self, name):
        self.nsem += 1
        return self.stack.enter_context(self.nc.semaphore(name))

    def sb(self, name, shape, dt):
        return self.stack.enter_context(self.nc.sbuf_tensor(name, list(shape), dt))

    def ps(self, name, shape, dt):
        return self.stack.enter_context(self.nc.psum_tensor(name, list(shape), dt))

    def _wait(self, E, tik):
        if tik is None:
            return
        if E.is_pe and tik.sem is E.sem:
            return
        assert tik.val is not None, "waiting on an unsignaled instruction"
        if E.waited.get(tik.sem, 0) >= tik.val:
            return
        E.eng.wait_ge(tik.sem, tik.val)
        E.waited[tik.sem] = tik.val

    def _sync(self, E, reads, writes, waw=True):
        for d in reads:
            self._wait(E, d.w)
        for d in writes:
            if waw:
                self._wait(E, d.w)
            for t in list(d.r.values()):
                self._wait(E, t)

    def op(self, E, fn, reads=(), writes=(), signal=True):
        self._sync(E, reads, writes)
        ins = fn()
        self.nins += 1
        if signal:
            E.cnt += 1
            ins.then_inc(E.sem, 1)
            E.pending.val = E.cnt
            tik = E.pending
            E.pending = Tik(E.sem, None)
        else:
            tik = E.pending
        for d in reads:
            d.r[tik.sem] = tik
        for d in writes:
            d.w = tik
            d.r = {}
        return tik

    def dma(self, Q, out, in_, reads=(), writes=(), owner=None, waw=True, **kw):
        self._sync(Q, reads, writes, waw=waw)
        if owner is None:
            owner = writes[0] if writes else reads[0]
        if owner.dsem is None:
            owner.dsem = self.new_sem("d_" + owner.name)
            self.owners.append(owner)
        owner.dcnt += 16
        Q.eng.dma_start(out=out, in_=in_, **kw).then_inc(owner.dsem, 16)
        self.nins += 1
        tik = Tik(owner.dsem, owner.dcnt)
        for d in reads:
            d.r[tik.sem] = tik
        for d in writes:
            d.w = tik
            d.r = {}
        return tik


class T:
    def __init__(self, t, name):
        self.t = t
        self.d = Dep(name)


def build_program(S, nphase=4):
    nc = bass.Bass("TRN2", target_bir_lowering=False)
    NT = S // 128
    NSB = S // 512

    def din(name, shape, dt=F32):
        return nc.dram_tensor(name, list(shape), dt, kind="ExternalInput").ap()

    x = din("x", [S, D])
    positions = din("positions", [S], I32)
    attn_norm = din("attn_norm", [DEPTH, D])
    w_in = din("w_in", [DEPTH, D, DIN])
    sgu_v_norm = din("sgu_v_norm", [DEPTH, 512])
    sgu_w = din("sgu_w_spatial", [DEPTH, 4, 128, 128])
    sgu_b = din("sgu_b_spatial", [DEPTH, 4, 128])
    lq1 = din("lambda_q1", [DEPTH, 64])
    lk1 = din("lambda_k1", [DEPTH, 64])
    lq2 = din("lambda_q2", [DEPTH, 64])
    lk2 = din("lambda_k2", [DEPTH, 64])
    subln = din("subln_gain", [DEPTH, 128])
    w_out = din("w_out", [DEPTH, D, D])
    ffn_norm = din("ffn_norm", [DEPTH, D])
    w_up = din("w_up", [DEPTH, D, 2 * DFF])
    conv_w = din("conv_w", [DEPTH, 3, 2 * DFF])
    conv_b = din("conv_b", [DEPTH, 2 * DFF])
    w_down = din("w_down", [DEPTH, DFF, D])
    final_norm = din("final_norm", [D])
    out = nc.dram_tensor("out", [S, D], F32, kind="ExternalOutput").ap()
    hs = nc.dram_tensor("hs", [S, D], F32).ap()

    kb = KB(nc)
    PE, ACT, DVE, POOL, SP = kb.pe, kb.act, kb.dve, kb.pool, kb.sp
    hdep = [Dep("hs%d" % t) for t in range(NT)]
    odep = [Dep("out%d" % t) for t in range(NT)]
    cdep = Dep("const")

    with kb.stack:
        ident = T(kb.sb("ident", [128, 128], BF16), "ident")
        maskT = T(kb.sb("maskT", [128, 128], BF16), "maskT")
        cosT = T(kb.sb("cosT", [128, NT, 8], F32), "cos")
        sinT = T(kb.sb("sinT", [128, NT, 8], F32), "sin")

        kb.op(POOL, lambda: nc.gpsimd.memset(ident.t[:], 1.0), writes=[ident.d])
        kb.op(POOL, lambda: nc.gpsimd.affine_select(
            out=ident.t[:], in_=ident.t[:], pattern=[[-1, 128]], compare_op=ALU.is_equal,
            fill=0.0, base=0, channel_multiplier=1), reads=[ident.d], writes=[ident.d])
        kb.op(POOL, lambda: nc.gpsimd.memset(maskT.t[:], 1.0), writes=[maskT.d])
        kb.op(POOL, lambda: nc.gpsimd.affine_select(
            out=maskT.t[:], in_=maskT.t[:], pattern=[[1, 128]], compare_op=ALU.is_ge,
            fill=0.0, base=0, channel_multiplier=-1), reads=[maskT.d], writes=[maskT.d])

        with contextlib.ExitStack() as tmp:
            posi = T(tmp.enter_context(nc.sbuf_tensor("posi", [128, NT], I32)), "posi")
            posf = T(tmp.enter_context(nc.sbuf_tensor("posf", [128, NT], F32)), "posf")
            ang = T(tmp.enter_context(nc.sbuf_tensor("ang", [128, NT, 8], F32)), "ang")
            red = T(tmp.enter_context(nc.sbuf_tensor("red", [128, NT, 8], F32)), "red")
            negpi = T(tmp.enter_context(nc.sbuf_tensor("negpi", [128, 1], F32)), "negpi")
            kb.dma(SP, posi.t[:], positions.rearrange("(t p) -> p t", p=128), writes=[posi.d],
                   allow_slow_non_contiguous=True)
            kb.op(DVE, lambda: nc.vector.tensor_copy(out=posf.t[:], in_=posi.t[:]),
                  reads=[posi.d], writes=[posf.d])
            kb.op(DVE, lambda: nc.vector.memset(negpi.t[:], -math.pi), writes=[negpi.d])
            for j in range(8):
                invf = float(np.float32(ROPE_THETA) ** np.float32(-(2.0 * j) / 16.0))
                kb.op(DVE, lambda j=j, invf=invf: nc.vector.tensor_scalar(
                    out=ang.t[:, :, j], in0=posf.t[:], scalar1=invf, scalar2=None, op0=ALU.mult),
                    reads=[posf.d], writes=[ang.d])
            ki = T(tmp.enter_context(nc.sbuf_tensor("ki", [128, NT, 8], I32)), "ki")
            kf = T(tmp.enter_context(nc.sbuf_tensor("kf", [128, NT, 8], F32)), "kf")
            TWO_PI = 2.0 * math.pi
            for (dst, shift) in ((sinT, 0.0), (cosT, 0.5 * math.pi)):
                kb.op(DVE, lambda shift=shift: nc.vector.tensor_scalar(
                    out=red.t[:], in0=ang.t[:], scalar1=shift, scalar2=None, op0=ALU.add),
                    reads=[ang.d], writes=[red.d])
                kb.op(DVE, lambda: nc.vector.tensor_scalar(
                    out=kf.t[:], in0=red.t[:], scalar1=1.0 / TWO_PI, scalar2=None, op0=ALU.mult),
                    reads=[red.d], writes=[kf.d])
                kb.op(DVE, lambda: nc.vector.tensor_copy(out=ki.t[:], in_=kf.t[:]), reads=[kf.d], writes=[ki.d])
                kb.op(DVE, lambda: nc.vector.tensor_copy(out=kf.t[:], in_=ki.t[:]), reads=[ki.d], writes=[kf.d])
                kb.op(DVE, lambda: nc.vector.scalar_tensor_tensor(
                    out=red.t[:], in0=kf.t[:], scalar=-TWO_PI, in1=red.t[:], op0=ALU.mult, op1=ALU.add),
                    reads=[kf.d, red.d], writes=[red.d])
                kb.op(DVE, lambda: nc.vector.tensor_scalar(
                    out=kf.t[:], in0=red.t[:], scalar1=math.pi, scalar2=TWO_PI, op0=ALU.is_gt, op1=ALU.mult),
                    reads=[red.d], writes=[kf.d])
                kb.op(DVE, lambda: nc.vector.tensor_tensor(out=red.t[:], in0=red.t[:], in1=kf.t[:], op=ALU.subtract),
                      reads=[red.d, kf.d], writes=[red.d])
                kb.op(DVE, lambda: nc.vector.tensor_scalar(
                    out=kf.t[:], in0=red.t[:], scalar1=-math.pi, scalar2=TWO_PI, op0=ALU.is_lt, op1=ALU.mult),
                    reads=[red.d], writes=[kf.d])
                kb.op(DVE, lambda: nc.vector.tensor_tensor(out=red.t[:], in0=red.t[:], in1=kf.t[:], op=ALU.add),
                      reads=[red.d, kf.d], writes=[red.d])
                kb.op(DVE, lambda: nc.vector.tensor_scalar(
                    out=red.t[:], in0=red.t[:], scalar1=-3.1415925, scalar2=3.1415925, op0=ALU.max, op1=ALU.min),
                    reads=[red.d], writes=[red.d])
                kb.op(ACT, lambda dst=dst: nc.scalar.activation(out=dst.t[:], in_=red.t[:], func=AF.Sin),
                      reads=[red.d], writes=[dst.d])
            kb.op(DVE, lambda: nc.vector.memset(negpi.t[:], 0.0), reads=[red.d, ang.d, posf.d, posi.d, ki.d, kf.d],
                  writes=[negpi.d, red.d, ang.d, posf.d, posi.d])
        phase_barrier(nc, kb)

        phases = [("mixer", 0), ("ffn", 0), ("mixer", 1), ("ffn", 1)][:nphase]
        for pi_, (kind, l) in enumerate(phases):
            last = pi_ == len(phases) - 1
            src = x if pi_ == 0 else hs
            if kind == "mixer":
                dst, ddeps = (out, odep) if last else (hs, hdep)
                emit_mixer(nc, kb, l, S, src, hdep, dst, ddeps, cdep, locals())
            else:
                fin = last and nphase == 4
                dst, ddeps = (out, odep) if last else (hs, hdep)
                emit_ffn(nc, kb, l, S, src, hdep, dst, ddeps, cdep, locals(), fin)

        for d in odep:
            kb._wait(SP, d.w)
    return nc


def bcast_load(kb, dst, src_row, n, cdep):
    kb.dma(kb.sp, dst.t[:, 0:n], src_row.partition_broadcast(128), writes=[dst.d], owner=cdep)


def emit_norm_T(nc, kb, G, src_tile_ap, slot, gT, nT, col0, hdep_r):
    PE, ACT, DVE, POOL, SP = kb.pe, kb.act, kb.dve, kb.pool, kb.sp
    hA, nb, ss, PS_T, ident = G["hA"][slot], G["nb"][slot], G["ss"][slot], G["PS_T"], G["ident"]
    kb.dma(SP, hA.t[:], src_tile_ap, reads=[hdep_r], writes=[hA.d])
    kb.op(ACT, lambda: nc.scalar.activation(out=nb.t[:], in_=hA.t[:], func=AF.Square,
                                            accum_out=ss.t[:, 0:1]),
          reads=[hA.d], writes=[nb.d, ss.d])
    kb.op(ACT, lambda: nc.scalar.activation(out=ss.t[:, 1:2], in_=ss.t[:, 0:1], func=AF.Sqrt,
                                            bias=G["epsc"].t[:, 0:1], scale=1.0 / D),
          reads=[ss.d, G["epsc"].d], writes=[ss.d])
    kb.op(DVE, lambda: nc.vector.reciprocal(out=ss.t[:, 2:3], in_=ss.t[:, 1:2]),
          reads=[ss.d], writes=[ss.d])
    kb.op(DVE, lambda: nc.vector.scalar_tensor_tensor(
        out=nb.t[:], in0=hA.t[:], scalar=ss.t[:, 2:3], in1=gT.t[:], op0=ALU.mult, op1=ALU.mult),
        reads=[hA.d, ss.d, gT.d], writes=[nb.d])
    for kc in range(8):
        kb.op(PE, lambda kc=kc: nc.tensor.transpose(
            PS_T.t[:, kc * 128:(kc + 1) * 128], nb.t[:, kc * 128:(kc + 1) * 128], ident.t[:]),
            reads=[nb.d, ident.d], writes=[PS_T.d], signal=(kc == 7))
    kb.op(DVE, lambda: nc.vector.tensor_copy(
        out=nT.t[:, :, col0:col0 + 128], in_=PS_T.t[:].rearrange("p (k c) -> p k c", k=8)),
        reads=[PS_T.d], writes=[nT.d])


def emit_mixer(nc, kb, l, S, src, hdep, dst, ddeps, cdep, env):
    PE, ACT, DVE, POOL, SP = kb.pe, kb.act, kb.dve, kb.pool, kb.sp
    NT, NSB = S // 128, S // 512
    ident, maskT, cosT, sinT = env["ident"], env["maskT"], env["cosT"], env["sinT"]
    lambda_init = 0.8 - 0.6 * math.exp(-0.3 * l)
    with contextlib.ExitStack() as st:
        kb.phase_id = getattr(kb, "phase_id", 0) + 1
        pfx = "p%d_" % kb.phase_id

        def sb(name, shape, dt):
            return T(st.enter_context(nc.sbuf_tensor(pfx + name, list(shape), dt)), pfx + name)

        def ps(name, shape, dt):
            return T(st.enter_context(nc.psum_tensor(pfx + name, list(shape), dt)), pfx + name)

        wi = sb("wi", [128, 8, DIN], BF16)
        wo = sb("wo", [128, 8, D], BF16)
        for kc in range(8):
            kb.dma(POOL, wi.t[:, kc, :], env["w_in"][l, kc * 128:(kc + 1) * 128, :], writes=[wi.d], waw=False)
        for kc in range(8):
            kb.dma(POOL, wo.t[:, kc, :], env["w_out"][l, kc * 128:(kc + 1) * 128, :], writes=[wo.d], waw=False)
        kT = sb("kT", [128, NH, S], BF16)
        vaug = sb("vaug", [128, NT, NH, VW], BF16)
        nT = sb("nT", [128, 8, 512], BF16)
        uT = sb("uT", [128, 4, 512], F32)
        vg = sb("vg", [128, 4, 512], BF16)
        qT = sb("qT", [128, NH, 512], BF16)
        mixT = nT
        otok = sb("otok", [128, 4, 512], BF16)
        G = dict(ident=ident)
        G["hA"] = [sb("hA%d" % i, [128, D], F32) for i in range(2)]
        G["nb"] = [sb("nb%d" % i, [128, D], BF16) for i in range(2)]
        G["ss"] = [sb("ss%d" % i, [128, 4], F32) for i in range(2)]
        hE = [sb("hE%d" % i, [128, D], F32) for i in range(2)]
        vtmp = [sb("vtmp%d" % i, [128, 512], F32) for i in range(1)] * 2
        vst = [sb("vst%d" % i, [128, 16], F32) for i in range(2)]
        qkb = [sb("qkb%d" % i, [128, 2, 512], BF16) for i in range(2)]
        rtmp = [sb("rtmp%d" % i, [128, 4, 16, 8], F32) for i in range(1)] * 2
        PT = [sb("PT%d" % i, [128, 2, 512], BF16) for i in range(3)]
        oacc = sb("oacc", [128, 8, 129], F32)
        o1 = sb("o1", [128, 4, 128], F32)
        o2 = sb("o2", [128, 4, 128], F32)
        ost = sb("ost", [128, 24], F32)
        sgt = [sb("sgt%d" % i, [128, 512], F32) for i in range(1)] * 2
        gA = sb("gA", [128, D], F32)
        vgain = sb("vgain", [128, 512], F32)
        subg = sb("subg", [128, 128], F32)
        bb = sb("bb", [128, 512], F32)
        lams = sb("lams", [128, 8], F32)
        epsc = sb("epsc", [128, 1], F32)
        G["epsc"] = epsc
        wT = sb("wT", [128, 4, 128], BF16)
        junk = sb("junk", [128, 128], F32)

        PS_X = [ps("PS_X%d" % i, [128, 1024], F32) for i in range(2)]
        PS_O = [ps("PS_O%d" % i, [128, 512], F32) for i in range(3)]
        PS_T = ps("PS_T", [128, 1024], BF16)
        G["PS_T"] = PS_T
        XD = [[Dep("x%d_%d" % (i, j)) for j in range(2)] for i in range(2)]

        lamt = T(o2.t, "lamt_alias")
        lamt.t = o2.t[:].rearrange("p a (b c) -> p (a b) c", c=64)[:, 0:4, :]
        wnat = o1
        wnb = T(PT[0].t[:].rearrange("p m q -> p (m q)")[:, 0:512].rearrange("p (g c) -> p g c", g=4), "wnb_alias")
        wnb.d = PT[0].d
        lamt.d = o2.d
        kb.op(DVE, lambda: nc.vector.memset(epsc.t[:], EPS), writes=[epsc.d])
        bcast_load(kb, gA, env["attn_norm"][l], D, cdep)
        bcast_load(kb, vgain, env["sgu_v_norm"][l], 512, cdep)
        bcast_load(kb, subg, env["subln"][l], 128, cdep)
        bcast_load(kb, bb, env["sgu_b"][l].rearrange("g p -> (g p)"), 512, cdep)
        for i, nm in enumerate(("lq1", "lk1", "lq2", "lk2")):
            kb.dma(SP, lamt.t[:, i, :], env[nm][l].partition_broadcast(128), writes=[lamt.d], owner=cdep,
                   waw=False)
        kb.dma(SP, wnat.t[:], env["sgu_w"][l].rearrange("g p q -> p g q"), writes=[wnat.d], owner=cdep)
        kb.op(DVE, lambda: nc.vector.tensor_scalar(out=subg.t[:], in0=subg.t[:], scalar1=1.0 - lambda_init,
                                                   scalar2=None, op0=ALU.mult),
              reads=[subg.d], writes=[subg.d])
        kb.op(DVE, lambda: nc.vector.tensor_tensor(out=lamt.t[:, 0, :], in0=lamt.t[:, 0, :], in1=lamt.t[:, 1, :],
                                                   op=ALU.mult), reads=[lamt.d], writes=[lamt.d])
        kb.op(DVE, lambda: nc.vector.tensor_tensor(out=lamt.t[:, 2, :], in0=lamt.t[:, 2, :], in1=lamt.t[:, 3, :],
                                                   op=ALU.mult), reads=[lamt.d], writes=[lamt.d])
        kb.op(DVE, lambda: nc.vector.reduce_sum(out=lams.t[:, 0:1], in_=lamt.t[:, 0, :], axis=AX.X),
              reads=[lamt.d], writes=[lams.d])
        kb.op(DVE, lambda: nc.vector.reduce_sum(out=lams.t[:, 1:2], in_=lamt.t[:, 2, :], axis=AX.X),
              reads=[lamt.d], writes=[lams.d])
        kb.op(ACT, lambda: nc.scalar.activation(out=lams.t[:, 2:4], in_=lams.t[:, 0:2], func=AF.Exp),
              reads=[lams.d], writes=[lams.d])
        kb.op(DVE, lambda: nc.vector.scalar_tensor_tensor(
            out=lams.t[:, 4:5], in0=lams.t[:, 3:4], scalar=-lambda_init, in1=lams.t[:, 2:3],
            op0=ALU.add, op1=ALU.subtract), reads=[lams.d], writes=[lams.d])
        kb.op(POOL, lambda: nc.gpsimd.affine_select(
            out=wnb.t[:], in_=wnat.t[:], pattern=[[0, 4], [-1, 128]], compare_op=ALU.is_ge,
            fill=0.0, base=0, channel_multiplier=1), reads=[wnat.d], writes=[wnb.d])
        for g in range(4):
            kb.op(PE, lambda g=g: nc.tensor.transpose(PS_T.t[:, g * 128:(g + 1) * 128], wnb.t[:, g, :], ident.t[:]),
                  reads=[wnb.d, ident.d], writes=[PS_T.d], signal=(g == 3))
        kb.op(DVE, lambda: nc.vector.tensor_copy(out=wT.t[:], in_=PS_T.t[:, 0:512].rearrange("p (g c) -> p g c", g=4)),
              reads=[PS_T.d], writes=[wT.d])
        kb.op(POOL, lambda: nc.gpsimd.memset(vaug.t[:, :, :, 128:130], 1.0), writes=[vaug.d])

        for sbi in range(NSB):
            t0 = sbi * 4
            for tt in range(4):
                t = t0 + tt
                emit_norm_T(nc, kb, G, src[t * 128:(t + 1) * 128, :], t % 2, gA, nT, tt * 128, hdep[t])
            for tt in range(4):
                t = t0 + tt
                cols = slice(tt * 128, (tt + 1) * 128)
                targets = [(PS_X[0].t[:, 0:512], 512, XD[0][0]), (PS_X[1].t[:, 0:512], 1024, XD[1][0]),
                           (PS_X[1].t[:, 512:1024], 1536, XD[1][1]), (PS_X[0].t[:, 512:1024], 2048, XD[0][1])]
                for kc in range(8):
                    for (o_ap, c0, dd) in targets:
                        kb.op(PE, lambda kc=kc, o_ap=o_ap, c0=c0: nc.tensor.matmul(
                            o_ap, nT.t[:, kc, cols], wi.t[:, kc, c0:c0 + 512], start=(kc == 0), stop=(kc == 7)),
                            reads=[nT.d, wi.d], writes=[dd], signal=(kc == 7))
                vt, vs = vtmp[t % 2], vst[t % 2]
                kb.op(ACT, lambda: nc.scalar.activation(out=vt.t[:], in_=PS_X[0].t[:, 0:512], func=AF.Gelu_apprx_tanh),
                      reads=[XD[0][0]], writes=[vt.d])
                for g in range(4):
                    kb.op(ACT, lambda g=g: nc.scalar.activation(
                        out=junk.t[:], in_=vt.t[:, g * 128:(g + 1) * 128], func=AF.Square,
                        accum_out=vs.t[:, g:g + 1]), reads=[vt.d], writes=[junk.d, vs.d])
                kb.op(ACT, lambda: nc.scalar.activation(out=vs.t[:, 4:8], in_=vs.t[:, 0:4], func=AF.Sqrt,
                                                        bias=epsc.t[:, 0:1], scale=1.0 / 128),
                      reads=[vs.d, epsc.d], writes=[vs.d])
                kb.op(DVE, lambda: nc.vector.reciprocal(out=vs.t[:, 8:12], in_=vs.t[:, 4:8]), reads=[vs.d], writes=[vs.d])
                for g in range(4):
                    kb.op(DVE, lambda g=g: nc.vector.scalar_tensor_tensor(
                        out=vg.t[:, tt, g * 128:(g + 1) * 128], in0=vt.t[:, g * 128:(g + 1) * 128],
                        scalar=vs.t[:, 8 + g:9 + g], in1=vgain.t[:, g * 128:(g + 1) * 128],
                        op0=ALU.mult, op1=ALU.mult), reads=[vt.d, vs.d, vgain.d], writes=[vg.d])
                kb.op(ACT, lambda: nc.scalar.activation(
                    out=vaug.t[:, t, :, 0:128], in_=PS_X[0].t[:, 512:1024].rearrange("p (h d) -> p h d", h=4),
                    func=AF.Copy), reads=[XD[0][1]], writes=[vaug.d])
                qk, rt = qkb[t % 2], rtmp[t % 2]
                psqk = PS_X[1].t[:].rearrange("p (a d) -> p a d", d=64)
                kb.op(ACT, lambda: nc.scalar.activation(out=qk.t[:].rearrange("p a c -> p (a c)"), in_=PS_X[1].t[:],
                                                        func=AF.Copy), reads=XD[1], writes=[qk.d])
                cb_ = cosT.t[:, t:t + 1, :].broadcast_to([128, 16, 8])
                sb_ = sinT.t[:, t:t + 1, :].broadcast_to([128, 16, 8])
                t1, t2 = psqk[:, :, 0:8], psqk[:, :, 8:16]
                for i, (a, b) in enumerate(((t1, cb_), (t2, sb_), (t2, cb_), (t1, sb_))):
                    kb.op(DVE, lambda i=i, a=a, b=b: nc.vector.tensor_tensor(out=rt.t[:, i], in0=a, in1=b, op=ALU.mult),
                          reads=XD[1] + [cosT.d, sinT.d], writes=[rt.d])
                qkv = qk.t[:].rearrange("p a (b d) -> p (a b) d", d=64)
                kb.op(DVE, lambda: nc.vector.tensor_tensor(out=qkv[:, :, 0:8], in0=rt.t[:, 0], in1=rt.t[:, 1],
                                                           op=ALU.subtract), reads=[rt.d], writes=[qk.d])
                kb.op(DVE, lambda: nc.vector.tensor_tensor(out=qkv[:, :, 8:16], in0=rt.t[:, 2], in1=rt.t[:, 3],
                                                           op=ALU.add), reads=[rt.d], writes=[qk.d])
                for a in range(2):
                    for h in range(4):
                        i = a * 4 + h
                        kb.op(PE, lambda a=a, h=h, i=i: nc.tensor.transpose(
                            PS_T.t[:, i * 128:(i + 1) * 128], qk.t[:, a, h * 128:(h + 1) * 128], ident.t[:]),
                            reads=[qk.d, ident.d], writes=[PS_T.d], signal=(i == 7))
                kb.op(DVE, lambda: nc.vector.tensor_copy(
                    out=qT.t[:, :, cols], in_=PS_T.t[:, 0:512].rearrange("p (h c) -> p h c", h=4)),
                    reads=[PS_T.d], writes=[qT.d])
                kb.op(DVE, lambda: nc.vector.tensor_copy(
                    out=kT.t[:, :, t * 128:(t + 1) * 128], in_=PS_T.t[:, 512:1024].rearrange("p (h c) -> p h c", h=4)),
                    reads=[PS_T.d], writes=[kT.d])
            for g in range(4):
                po = PS_O[g % 2]
                for kc in range(8):
                    kb.op(PE, lambda kc=kc, g=g, po=po: nc.tensor.matmul(
                        po.t[:], wi.t[:, kc, g * 128:(g + 1) * 128], nT.t[:, kc, :], start=(kc == 0), stop=(kc == 7)),
                        reads=[nT.d, wi.d], writes=[po.d], signal=(kc == 7))
                kb.op(ACT, lambda g=g, po=po: nc.scalar.activation(out=uT.t[:, g, :], in_=po.t[:], func=AF.Gelu_apprx_tanh),
                      reads=[po.d], writes=[uT.d])
            for tt in range(4):
                cols = slice(tt * 128, (tt + 1) * 128)
                po = PS_O[2]
                for g in range(4):
                    kb.op(PE, lambda g=g: nc.tensor.matmul(
                        po.t[:, g * 128:(g + 1) * 128], vg.t[:, tt, g * 128:(g + 1) * 128], wT.t[:, g, :],
                        start=True, stop=True), reads=[vg.d, wT.d], writes=[po.d], signal=(g == 3))
                sg = sgt[tt % 2]
                kb.op(DVE, lambda: nc.vector.tensor_tensor(out=sg.t[:], in0=po.t[:], in1=bb.t[:], op=ALU.add),
                      reads=[po.d, bb.d], writes=[sg.d])
                kb.op(POOL, lambda: nc.gpsimd.tensor_tensor(
                    out=mixT.t[:, 0:4, cols], in0=sg.t[:].rearrange("p (g c) -> p g c", g=4), in1=uT.t[:, :, cols],
                    op=ALU.mult), reads=[sg.d, uT.d], writes=[mixT.d])
            nfull = sbi * 4
            items = [(h, c) for h in range(NH) for c in range(nfull + 4)]

            def emit_qk(idx):
                h, c = items[idx]
                X = PS_X[idx % 2]
                j = max(0, c - nfull)
                q0 = j * 128
                for m in range(2):
                    kb.op(PE, lambda m=m: nc.tensor.matmul(
                        X.t[:, m * 512 + q0:(m + 1) * 512], kT.t[m * 64:(m + 1) * 64, h, c * 128:(c + 1) * 128],
                        qT.t[m * 64:(m + 1) * 64, h, q0:512], start=True, stop=True),
                        reads=[kT.d, qT.d], writes=XD[idx % 2], signal=(m == 1))

            emit_qk(0)
            for idx, (h, c) in enumerate(items):
                if idx + 1 < len(items):
                    emit_qk(idx + 1)
                X = PS_X[idx % 2]
                P = PT[idx % 3]
                j = max(0, c - nfull)
                q0 = j * 128
                kb.op(ACT, lambda: nc.scalar.activation(
                    out=P.t[:, :, q0:512], in_=X.t[:].rearrange("p (m q) -> p m q", m=2)[:, :, q0:512],
                    func=AF.Exp, scale=0.125), reads=XD[idx % 2], writes=[P.d])
                if c >= nfull:
                    kb.op(POOL, lambda: nc.gpsimd.tensor_tensor(
                        out=P.t[:, :, q0:q0 + 128], in0=P.t[:, :, q0:q0 + 128],
                        in1=maskT.t[:].unsqueeze(1).broadcast_to([128, 2, 128]), op=ALU.mult),
                        reads=[P.d, maskT.d], writes=[P.d])
                qbs = list(range(j, 4))
                for qb in qbs:
                    for m in range(2):
                        a = qb * 2 + m
                        bank, slot = a // 3, a % 3
                        kb.op(PE, lambda qb=qb, m=m, bank=bank, slot=slot: nc.tensor.matmul(
                            PS_O[bank].t[:, slot * 129:(slot + 1) * 129], P.t[:, m, qb * 128:(qb + 1) * 128],
                            vaug.t[:, c, h, 0:129], start=(c == 0 and slot == 0), stop=(c == nfull + qb),
                            skip_group_check=True),
                            reads=[P.d, vaug.d], writes=[PS_O[bank].d], signal=(qb == 3 and m == 1))
                if c == nfull + 3:
                    for bank in range(3):
                        n = 3 if bank < 2 else 2
                        kb.op(DVE, lambda bank=bank, n=n: nc.vector.tensor_copy(
                            out=oacc.t[:, bank * 3:bank * 3 + n, :].rearrange("p a c -> p (a c)"),
                            in_=PS_O[bank].t[:, 0:n * 129]), reads=[PS_O[bank].d], writes=[oacc.d])
                    ov = oacc.t[:].rearrange("p (q m) c -> p q m c", m=2)
                    kb.op(DVE, lambda: nc.vector.reciprocal(out=ost.t[:, 0:8], in_=oacc.t[:, :, 128]),
                          reads=[oacc.d], writes=[ost.d])
                    osv = ost.t[:, 0:8].rearrange("p (q m) -> p q m", m=2)
                    kb.op(DVE, lambda: nc.vector.tensor_scalar(out=ost.t[:, 8:12], in0=osv[:, :, 1], scalar1=lams.t[:, 4:5],
                                                               scalar2=None, op0=ALU.mult),
                          reads=[ost.d, lams.d], writes=[ost.d])
                    kb.op(DVE, lambda: nc.vector.tensor_tensor(
                        out=o1.t[:], in0=ov[:, :, 0, 0:128], in1=osv[:, :, 0:1].broadcast_to([128, 4, 128]), op=ALU.mult),
                        reads=[oacc.d, ost.d], writes=[o1.d])
                    kb.op(DVE, lambda: nc.vector.tensor_tensor(
                        out=o2.t[:], in0=ov[:, :, 1, 0:128], in1=ost.t[:, 8:12].unsqueeze(2).broadcast_to([128, 4, 128]),
                        op=ALU.mult), reads=[oacc.d, ost.d], writes=[o2.d])
                    kb.op(DVE, lambda: nc.vector.tensor_tensor(out=o1.t[:], in0=o1.t[:], in1=o2.t[:], op=ALU.add),
                          reads=[o1.d, o2.d], writes=[o1.d])
                    kb.op(DVE, lambda: nc.vector.tensor_tensor(out=o2.t[:], in0=o1.t[:], in1=o1.t[:], op=ALU.mult),
                          reads=[o1.d], writes=[o2.d])
                    kb.op(DVE, lambda: nc.vector.reduce_sum(out=ost.t[:, 12:16], in_=o2.t[:], axis=AX.X),
                          reads=[o2.d], writes=[ost.d])
                    kb.op(ACT, lambda: nc.scalar.activation(out=ost.t[:, 16:20], in_=ost.t[:, 12:16], func=AF.Sqrt,
                                                            bias=epsc.t[:, 0:1], scale=1.0 / 128),
                          reads=[ost.d, epsc.d], writes=[ost.d])
                    kb.op(DVE, lambda: nc.vector.reciprocal(out=ost.t[:, 20:24], in_=ost.t[:, 16:20]),
                          reads=[ost.d], writes=[ost.d])
                    kb.op(DVE, lambda: nc.vector.tensor_tensor(
                        out=o1.t[:], in0=o1.t[:], in1=ost.t[:, 20:24].unsqueeze(2).broadcast_to([128, 4, 128]), op=ALU.mult),
                        reads=[o1.d, ost.d], writes=[o1.d])
                    kb.op(DVE, lambda h=h: nc.vector.tensor_tensor(
                        out=otok.t[:, :, h * 128:(h + 1) * 128], in0=o1.t[:],
                        in1=subg.t[:].unsqueeze(1).broadcast_to([128, 4, 128]), op=ALU.mult),
                        reads=[o1.d, subg.d], writes=[otok.d])
            for tt in range(4):
                cols = slice(tt * 128, (tt + 1) * 128)
                for h in range(4):
                    kb.op(PE, lambda h=h: nc.tensor.transpose(
                        PS_T.t[:, h * 128:(h + 1) * 128], otok.t[:, tt, h * 128:(h + 1) * 128], ident.t[:]),
                        reads=[otok.d, ident.d], writes=[PS_T.d], signal=(h == 3))
                kb.op(DVE, lambda: nc.vector.tensor_copy(
                    out=mixT.t[:, 4:8, cols], in_=PS_T.t[:, 0:512].rearrange("p (h c) -> p h c", h=4)),
                    reads=[PS_T.d], writes=[mixT.d])
            for tt in range(4):
                t = t0 + tt
                cols = slice(tt * 128, (tt + 1) * 128)
                he = hE[t % 2]
                kb.dma(SP, he.t[:], src[t * 128:(t + 1) * 128, :], reads=[hdep[t]], writes=[he.d])
                for cb in range(2):
                    po = PS_O[cb]
                    for kc in range(8):
                        kb.op(PE, lambda kc=kc, cb=cb, po=po: nc.tensor.matmul(
                            po.t[:], mixT.t[:, kc, cols], wo.t[:, kc, cb * 512:(cb + 1) * 512],
                            start=(kc == 0), stop=(kc == 7)), reads=[mixT.d, wo.d], writes=[po.d], signal=(kc == 7))
                    kb.op(DVE, lambda cb=cb, po=po: nc.vector.tensor_tensor(
                        out=he.t[:, cb * 512:(cb + 1) * 512], in0=po.t[:], in1=he.t[:, cb * 512:(cb + 1) * 512], op=ALU.add),
                        reads=[po.d, he.d], writes=[he.d])
                kb.dma(SP, dst[t * 128:(t + 1) * 128, :], he.t[:], reads=[he.d], writes=[ddeps[t]], owner=he.d)
        phase_barrier(nc, kb)


def phase_barrier(nc, kb):
    engs = [kb.pe, kb.act, kb.dve, kb.pool, kb.sp]
    tiks = [Tik(E.sem, E.cnt) for E in engs if E.cnt > 0]
    tiks += [Tik(d.dsem, d.dcnt) for d in kb.owners if d.dcnt > 0]
    for E in engs:
        assert E.pending.val is None
        for t in tiks:
            kb._wait(E, t)


def emit_ffn(nc, kb, l, S, src, hdep, dst, ddeps, cdep, env, fin):
    PE, ACT, DVE, POOL, SP = kb.pe, kb.act, kb.dve, kb.pool, kb.sp
    NT, NSB = S // 128, S // 512
    ident = env["ident"]
    with contextlib.ExitStack() as st:
        kb.phase_id = getattr(kb, "phase_id", 0) + 1
        pfx = "p%d_" % kb.phase_id

        def sb(name, shape, dt):
            return T(st.enter_context(nc.sbuf_tensor(pfx + name, list(shape), dt)), pfx + name)

        def ps(name, shape, dt):
            return T(st.enter_context(nc.psum_tensor(pfx + name, list(shape), dt)), pfx + name)

        wu = sb("wu", [128, 8, 2 * DFF], BF16)
        wd = sb("wd", [128, NCG, D], BF16)
        for kc in range(8):
            for hh in range(2):
                kb.dma(POOL, wu.t[:, kc, hh * DFF:(hh + 1) * DFF],
                       env["w_up"][l, kc * 128:(kc + 1) * 128, hh * DFF:(hh + 1) * DFF], writes=[wu.d], waw=False)
        for cg in range(NCG):
            kb.dma(POOL, wd.t[:, cg, :], env["w_down"][l, cg * 128:(cg + 1) * 128, :], writes=[wd.d], waw=False)
        nT = sb("nT", [128, 8, 512], BF16)
        actT = sb("actT", [128, NCG, 512], BF16)
        G = dict(ident=ident)
        G["hA"] = [sb("hA%d" % i, [128, D], F32) for i in range(2)]
        G["nb"] = [sb("nb%d" % i, [128, D], BF16) for i in range(2)]
        G["ss"] = [sb("ss%d" % i, [128, 4], F32) for i in range(2)]
        hE = [sb("hE%d" % i, [128, D], F32) for i in range(2)]
        fs = [sb("fs%d" % i, [128, 4], F32) for i in range(2)]
        Ag = [sb("Ag%d" % i, [128, 512], F32) for i in range(2)]
        Av = [sb("Av%d" % i, [128, 512], F32) for i in range(2)]
        tails = sb("tails", [128, 2, 2 * NCG, 2], F32)
        tdep = [Dep("tail0"), Dep("tail1")]
        cw = sb("cw", [128, 3, 2 * NCG], F32)
        cbias = sb("cbias", [128, 2 * NCG], F32)
        gF = sb("gF", [128, D], F32)
        epsc = sb("epsc", [128, 1], F32)
        G["epsc"] = epsc
        PS_G = [ps("PS_G%d" % i, [128, 512], F32) for i in range(2)]
        PS_V = [ps("PS_V%d" % i, [128, 512], F32) for i in range(2)]
        PS_D = [ps("PS_D%d" % i, [128, 512], F32) for i in range(2)]
        PS_T = ps("PS_T", [128, 1024], BF16)
        G["PS_T"] = PS_T

        kb.op(DVE, lambda: nc.vector.memset(epsc.t[:], EPS), writes=[epsc.d])
        bcast_load(kb, gF, env["ffn_norm"][l], D, cdep)
        if fin:
            gfin = sb("gfin", [128, D], F32)
            bcast_load(kb, gfin, env["final_norm"], D, cdep)
        for j in range(3):
            kb.dma(SP, cw.t[:, j, :], env["conv_w"][l, j].rearrange("(g p) -> p g", p=128), writes=[cw.d], owner=cdep,
                   waw=False, allow_slow_non_contiguous=True)
        kb.dma(SP, cbias.t[:], env["conv_b"][l].rearrange("(g p) -> p g", p=128), writes=[cbias.d], owner=cdep,
               allow_slow_non_contiguous=True)

        for sbi in range(NSB):
            t0 = sbi * 4
            for tt in range(4):
                t = t0 + tt
                emit_norm_T(nc, kb, G, src[t * 128:(t + 1) * 128, :], t % 2, gF, nT, tt * 128, hdep[t])
            for cg in range(NCG):
                pg, pv = PS_G[cg % 2], PS_V[cg % 2]
                ag, av = Ag[cg % 2], Av[cg % 2]
                for (P, c0) in ((pg, cg * 128), (pv, DFF + cg * 128)):
                    for kc in range(8):
                        kb.op(PE, lambda P=P, c0=c0, kc=kc: nc.tensor.matmul(
                            P.t[:], wu.t[:, kc, c0:c0 + 128], nT.t[:, kc, :], start=(kc == 0), stop=(kc == 7)),
                            reads=[wu.d, nT.d], writes=[P.d], signal=(kc == 7))
                for (P, A, gi) in ((pg, ag, cg), (pv, av, NCG + cg)):
                    w0, w1, w2 = (cw.t[:, j, gi:gi + 1] for j in range(3))
                    kb.op(ACT, lambda P=P, A=A, gi=gi, w2=w2: nc.scalar.activation(
                        out=A.t[:], in_=P.t[:], func=AF.Identity, bias=cbias.t[:, gi:gi + 1], scale=w2),
                        reads=[P.d, cw.d, cbias.d], writes=[A.d])
                    kb.op(DVE, lambda P=P, A=A, w1=w1: nc.vector.scalar_tensor_tensor(
                        out=A.t[:, 1:512], in0=P.t[:, 0:511], scalar=w1, in1=A.t[:, 1:512], op0=ALU.mult, op1=ALU.add),
                        reads=[P.d, A.d, cw.d], writes=[A.d])
                    kb.op(DVE, lambda P=P, A=A, w0=w0: nc.vector.scalar_tensor_tensor(
                        out=A.t[:, 2:512], in0=P.t[:, 0:510], scalar=w0, in1=A.t[:, 2:512], op0=ALU.mult, op1=ALU.add),
                        reads=[P.d, A.d, cw.d], writes=[A.d])
                    if sbi > 0:
                        tl = tails.t[:, (sbi - 1) % 2, gi, :]
                        td = tdep[(sbi - 1) % 2]
                        kb.op(DVE, lambda A=A, tl=tl, w1=w1: nc.vector.scalar_tensor_tensor(
                            out=A.t[:, 0:1], in0=tl[:, 1:2], scalar=w1, in1=A.t[:, 0:1], op0=ALU.mult, op1=ALU.add),
                            reads=[td, A.d, cw.d], writes=[A.d])
                        kb.op(DVE, lambda A=A, tl=tl, w0=w0: nc.vector.scalar_tensor_tensor(
                            out=A.t[:, 0:2], in0=tl[:, 0:2], scalar=w0, in1=A.t[:, 0:2], op0=ALU.mult, op1=ALU.add),
                            reads=[td, A.d, cw.d], writes=[A.d])
                    if sbi < NSB - 1:
                        kb.op(ACT, lambda P=P, gi=gi: nc.scalar.activation(
                            out=tails.t[:, sbi % 2, gi, :], in_=P.t[:, 510:512], func=AF.Copy),
                            reads=[P.d], writes=[tdep[sbi % 2]])
                kb.op(ACT, lambda ag=ag: nc.scalar.activation(out=ag.t[:], in_=ag.t[:], func=AF.Silu),
                      reads=[ag.d], writes=[ag.d])
                kb.op(POOL, lambda ag=ag, av=av, cg=cg: nc.gpsimd.tensor_tensor(
                    out=actT.t[:, cg, :], in0=ag.t[:], in1=av.t[:], op=ALU.mult),
                    reads=[ag.d, av.d], writes=[actT.d])
            for tt in range(4):
                t = t0 + tt
                cols = slice(tt * 128, (tt + 1) * 128)
                he = hE[t % 2]
                kb.dma(SP, he.t[:], src[t * 128:(t + 1) * 128, :], reads=[hdep[t]], writes=[he.d])
                for cb in range(2):
                    po = PS_D[cb]
                    for cg in range(NCG):
                        kb.op(PE, lambda cg=cg, cb=cb, po=po: nc.tensor.matmul(
                            po.t[:], actT.t[:, cg, cols], wd.t[:, cg, cb * 512:(cb + 1) * 512],
                            start=(cg == 0), stop=(cg == NCG - 1)), reads=[actT.d, wd.d], writes=[po.d],
                            signal=(cg == NCG - 1))
                    kb.op(DVE, lambda cb=cb, po=po: nc.vector.tensor_tensor(
                        out=he.t[:, cb * 512:(cb + 1) * 512], in0=po.t[:], in1=he.t[:, cb * 512:(cb + 1) * 512], op=ALU.add),
                        reads=[po.d, he.d], writes=[he.d])
                if fin:
                    f = fs[t % 2]
                    nbj = G["nb"][t % 2]
                    kb.op(ACT, lambda: nc.scalar.activation(out=nbj.t[:], in_=he.t[:], func=AF.Square,
                                                            accum_out=f.t[:, 0:1]),
                          reads=[he.d], writes=[nbj.d, f.d])
                    kb.op(ACT, lambda: nc.scalar.activation(out=f.t[:, 1:2], in_=f.t[:, 0:1], func=AF.Sqrt,
                                                            bias=epsc.t[:, 0:1], scale=1.0 / D),
                          reads=[f.d, epsc.d], writes=[f.d])
                    kb.op(DVE, lambda: nc.vector.reciprocal(out=f.t[:, 2:3], in_=f.t[:, 1:2]), reads=[f.d], writes=[f.d])
                    kb.op(DVE, lambda: nc.vector.scalar_tensor_tensor(
                        out=he.t[:], in0=he.t[:], scalar=f.t[:, 2:3], in1=gfin.t[:], op0=ALU.mult, op1=ALU.mult),
                        reads=[he.d, f.d, gfin.d], writes=[he.d])
                kb.dma(SP, dst[t * 128:(t + 1) * 128, :], he.t[:], reads=[he.d], writes=[ddeps[t]], owner=he.d)
        phase_barrier(nc, kb)


_CACHE = {}


def kernel(**inputs):
    S = 4096
    nb = 8
    if S not in _CACHE:
        _CACHE[S] = build_program(S, 4)
    nc = _CACHE[S]
    names = ["attn_norm", "w_in", "sgu_v_norm", "sgu_w_spatial", "sgu_b_spatial", "lambda_q1", "lambda_k1",
             "lambda_q2", "lambda_k2", "subln_gain", "w_out", "ffn_norm", "w_up", "conv_w", "conv_b", "w_down",
             "final_norm"]
    shared = {n: np.ascontiguousarray(np.asarray(inputs[n], dtype=np.float32)) for n in names}
    x = np.asarray(inputs["x"], dtype=np.float32)
    pos = np.asarray(inputs["positions"], dtype=np.int32)
    in_maps = []
    for b in range(nb):
        m = dict(shared)
        m["x"] = np.ascontiguousarray(x[b])
        m["positions"] = np.ascontiguousarray(pos[b])
        in_maps.append(m)
    res = run_bass_kernel_spmd(nc, in_maps, core_ids=list(range(nb)))
    return np.stack([np.asarray(r["out"], dtype=np.float32) for r in res.results], axis=0)
```

```python
import contextlib
import math
import numpy as np
import concourse.bass as bass
import concourse.mybir as mybir
from concourse.bass_utils import run_bass_kernel_spmd

F32 = mybir.dt.float32
BF16 = mybir.dt.bfloat16
I32 = mybir.dt.int32
AF = mybir.ActivationFunctionType
ALU = mybir.AluOpType
AX = mybir.AxisListType

D = 1024
DIN = 2560
DFF = 2816
NCG = DFF // 128
NH = 4
EPS = 1e-6
DEPTH = 2
ROPE_THETA = 500000.0
VW = 130


class Tik:
    __slots__ = ("sem", "val")

    def __init__(self, sem, val):
        self.sem = sem
        self.val = val


class Dep:
    __slots__ = ("name", "w", "r", "dsem", "dcnt", "shared", "excl")

    def __init__(self, name):
        self.name = name
        self.w = None
        self.r = {}
        self.dsem = None
        self.dcnt = 0
        self.shared = None
        self.excl = False


class Eng:
    def __init__(self, kb, name, eng, is_pe=False):
        self.kb = kb
        self.name = name
        self.eng = eng
        self.is_pe = is_pe
        self.sem = kb.new_sem("e_" + name)
        self.cnt = 0
        self.waited = {}
        self.pending = Tik(self.sem, None)


class KB:
    def __init__(self, nc):
        self.nc = nc
        self.stack = contextlib.ExitStack()
        self.nsem = 0
        self.pe = Eng(self, "pe", nc.tensor, True)
        self.act = Eng(self, "act", nc.scalar)
        self.dve = Eng(self, "dve", nc.vector)
        self.pool = Eng(self, "pool", nc.gpsimd)
        self.sp = Eng(self, "sp", nc.sync)
        self.nins = 0
        self.owners = []

    def new_sem(self, name):
        self.nsem += 1
        return self.stack.enter_context(self.nc.semaphore(name))

    def sb(self, name, shape, dt):
        return self.stack.enter_context(self.nc.sbuf_tensor(name, list(shape), dt))

    def ps(self, name, shape, dt):
        return self.stack.enter_context(self.nc.psum_tensor(name, list(shape), dt))

    def _wait(self, E, tik):
        if tik is None:
            return
        if E.is_pe and tik.sem is E.sem:
            return
        assert tik.val is not None, "waiting on an unsignaled instruction"
        if E.waited.get(tik.sem, 0) >= tik.val:
            return
        E.eng.wait_ge(tik.sem, tik.val)
        E.waited[tik.sem] = tik.val

    def _sync(self, E, reads, writes, waw=True):
        for d in reads:
            self._wait(E, d.w)
        for d in writes:
            if waw:
                self._wait(E, d.w)
            for t in list(d.r.values()):
                self._wait(E, t)

    def op(self, E, fn, reads=(), writes=(), signal=True):
        if any(d.excl for d in reads):
            writes = list(writes) + [d for d in reads if d.excl and d not in writes]
            reads = [d for d in reads if not d.excl]
        self._sync(E, reads, writes)
        ins = fn()
        self.nins += 1
        if signal:
            E.cnt += 1
            ins.then_inc(E.sem, 1)
            E.pending.val = E.cnt
            tik = E.pending
            E.pending = Tik(E.sem, None)
        else:
            tik = E.pending
        for d in reads:
            d.r[tik.sem] = tik
        for d in writes:
            d.w = tik
            d.r = {}
        return tik

    def dma(self, Q, out, in_, reads=(), writes=(), owner=None, waw=True, **kw):
        self._sync(Q, reads, writes, waw=waw)
        if owner is None:
            owner = writes[0] if writes else reads[0]
        if owner.dsem is None:
            owner.dsem = self.new_sem("d_" + owner.name)
            self.owners.append(owner)
        owner.dcnt += 16
        Q.eng.dma_start(out=out, in_=in_, **kw).then_inc(owner.dsem, 16)
        self.nins += 1
        if getattr(owner, "shared", None) is not None:
            tik = owner.shared
            tik.sem = owner.dsem
            tik.val = owner.dcnt
        else:
            tik = Tik(owner.dsem, owner.dcnt)
        for d in reads:
            d.r[tik.sem] = tik
        for d in writes:
            d.w = tik
            d.r = {}
        return tik


class T:
    def __init__(self, t, name):
        self.t = t
        self.d = Dep(name)


def build_program(S, nphase=4):
    nc = bass.Bass("TRN2", target_bir_lowering=False)
    NT = S // 128
    NSB = S // 512

    def din(name, shape, dt=F32):
        return nc.dram_tensor(name, list(shape), dt, kind="ExternalInput").ap()

    x = din("x", [S, D])
    positions = din("positions", [S], I32)
    attn_norm = din("attn_norm", [DEPTH, D])
    w_in = din("w_in", [DEPTH, D, DIN])
    sgu_v_norm = din("sgu_v_norm", [DEPTH, 512])
    sgu_w = din("sgu_w_spatial", [DEPTH, 4, 128, 128])
    sgu_b = din("sgu_b_spatial", [DEPTH, 4, 128])
    lq1 = din("lambda_q1", [DEPTH, 64])
    lk1 = din("lambda_k1", [DEPTH, 64])
    lq2 = din("lambda_q2", [DEPTH, 64])
    lk2 = din("lambda_k2", [DEPTH, 64])
    subln = din("subln_gain", [DEPTH, 128])
    w_out = din("w_out", [DEPTH, D, D])
    ffn_norm = din("ffn_norm", [DEPTH, D])
    w_up = din("w_up", [DEPTH, D, 2 * DFF])
    conv_w = din("conv_w", [DEPTH, 3, 2 * DFF])
    conv_b = din("conv_b", [DEPTH, 2 * DFF])
    w_down = din("w_down", [DEPTH, DFF, D])
    final_norm = din("final_norm", [D])
    out = nc.dram_tensor("out", [S, D], F32, kind="ExternalOutput").ap()
    hs = nc.dram_tensor("hs", [S, D], F32).ap()

    kb = KB(nc)
    PE, ACT, DVE, POOL, SP = kb.pe, kb.act, kb.dve, kb.pool, kb.sp
    hdep = [Dep("hs%d" % t) for t in range(NT)]
    odep = [Dep("out%d" % t) for t in range(NT)]
    cdep = Dep("const")
    cdep.shared = Tik(None, 0)

    with kb.stack:
        ident = T(kb.sb("ident", [128, 128], BF16), "ident")
        maskT = T(kb.sb("maskT", [128, 128], BF16), "maskT")
        cosT = T(kb.sb("cosT", [128, NT, 8], F32), "cos")
        sinT = T(kb.sb("sinT", [128, NT, 8], F32), "sin")

        kb.op(POOL, lambda: nc.gpsimd.memset(ident.t[:], 1.0), writes=[ident.d])
        kb.op(POOL, lambda: nc.gpsimd.affine_select(
            out=ident.t[:], in_=ident.t[:], pattern=[[-1, 128]], compare_op=ALU.is_equal,
            fill=0.0, base=0, channel_multiplier=1), reads=[ident.d], writes=[ident.d])
        kb.op(POOL, lambda: nc.gpsimd.memset(maskT.t[:], 1.0), writes=[maskT.d])
        kb.op(POOL, lambda: nc.gpsimd.affine_select(
            out=maskT.t[:], in_=maskT.t[:], pattern=[[1, 128]], compare_op=ALU.is_ge,
            fill=0.0, base=0, channel_multiplier=-1), reads=[maskT.d], writes=[maskT.d])

        with contextlib.ExitStack() as tmp:
            posi = T(tmp.enter_context(nc.sbuf_tensor("posi", [128, NT], I32)), "posi")
            posf = T(tmp.enter_context(nc.sbuf_tensor("posf", [128, NT], F32)), "posf")
            ang = T(tmp.enter_context(nc.sbuf_tensor("ang", [128, NT, 8], F32)), "ang")
            red = T(tmp.enter_context(nc.sbuf_tensor("red", [128, NT, 8], F32)), "red")
            negpi = T(tmp.enter_context(nc.sbuf_tensor("negpi", [128, 1], F32)), "negpi")
            kb.dma(SP, posi.t[:], positions.rearrange("(t p) -> p t", p=128), writes=[posi.d],
                   allow_slow_non_contiguous=True)
            kb.op(DVE, lambda: nc.vector.tensor_copy(out=posf.t[:], in_=posi.t[:]),
                  reads=[posi.d], writes=[posf.d])
            kb.op(DVE, lambda: nc.vector.memset(negpi.t[:], -math.pi), writes=[negpi.d])
            for j in range(8):
                invf = float(np.float32(ROPE_THETA) ** np.float32(-(2.0 * j) / 16.0))
                kb.op(DVE, lambda j=j, invf=invf: nc.vector.tensor_scalar(
                    out=ang.t[:, :, j], in0=posf.t[:], scalar1=invf, scalar2=None, op0=ALU.mult),
                    reads=[posf.d], writes=[ang.d])
            ki = T(tmp.enter_context(nc.sbuf_tensor("ki", [128, NT, 8], I32)), "ki")
            kf = T(tmp.enter_context(nc.sbuf_tensor("kf", [128, NT, 8], F32)), "kf")
            TWO_PI = 2.0 * math.pi
            for (dst, shift) in ((sinT, 0.0), (cosT, 0.5 * math.pi)):
                kb.op(DVE, lambda shift=shift: nc.vector.tensor_scalar(
                    out=red.t[:], in0=ang.t[:], scalar1=shift, scalar2=None, op0=ALU.add),
                    reads=[ang.d], writes=[red.d])
                kb.op(DVE, lambda: nc.vector.tensor_scalar(
                    out=kf.t[:], in0=red.t[:], scalar1=1.0 / TWO_PI, scalar2=None, op0=ALU.mult),
                    reads=[red.d], writes=[kf.d])
                kb.op(DVE, lambda: nc.vector.tensor_copy(out=ki.t[:], in_=kf.t[:]), reads=[kf.d], writes=[ki.d])
                kb.op(DVE, lambda: nc.vector.tensor_copy(out=kf.t[:], in_=ki.t[:]), reads=[ki.d], writes=[kf.d])
                kb.op(DVE, lambda: nc.vector.scalar_tensor_tensor(
                    out=red.t[:], in0=kf.t[:], scalar=-TWO_PI, in1=red.t[:], op0=ALU.mult, op1=ALU.add),
                    reads=[kf.d, red.d], writes=[red.d])
                kb.op(DVE, lambda: nc.vector.tensor_scalar(
                    out=kf.t[:], in0=red.t[:], scalar1=math.pi, scalar2=TWO_PI, op0=ALU.is_gt, op1=ALU.mult),
                    reads=[red.d], writes=[kf.d])
                kb.op(DVE, lambda: nc.vector.tensor_tensor(out=red.t[:], in0=red.t[:], in1=kf.t[:], op=ALU.subtract),
                      reads=[red.d, kf.d], writes=[red.d])
                kb.op(DVE, lambda: nc.vector.tensor_scalar(
                    out=kf.t[:], in0=red.t[:], scalar1=-math.pi, scalar2=TWO_PI, op0=ALU.is_lt, op1=ALU.mult),
                    reads=[red.d], writes=[kf.d])
                kb.op(DVE, lambda: nc.vector.tensor_tensor(out=red.t[:], in0=red.t[:], in1=kf.t[:], op=ALU.add),
                      reads=[red.d, kf.d], writes=[red.d])
                kb.op(DVE, lambda: nc.vector.tensor_scalar(
                    out=red.t[:], in0=red.t[:], scalar1=-3.1415925, scalar2=3.1415925, op0=ALU.max, op1=ALU.min),
                    reads=[red.d], writes=[red.d])
                kb.op(ACT, lambda dst=dst: nc.scalar.activation(out=dst.t[:], in_=red.t[:], func=AF.Sin),
                      reads=[red.d], writes=[dst.d])
            kb.op(DVE, lambda: nc.vector.memset(negpi.t[:], 0.0), reads=[red.d, ang.d, posf.d, posi.d, ki.d, kf.d],
                  writes=[negpi.d, red.d, ang.d, posf.d, posi.d])
        phase_barrier(nc, kb)

        phases = [("mixer", 0), ("ffn", 0), ("mixer", 1), ("ffn", 1)][:nphase]
        for pi_, (kind, l) in enumerate(phases):
            last = pi_ == len(phases) - 1
            src = x if pi_ == 0 else hs
            if kind == "mixer":
                dst, ddeps = (out, odep) if last else (hs, hdep)
                emit_mixer(nc, kb, l, S, src, hdep, dst, ddeps, cdep, locals())
            else:
                fin = last and nphase == 4
                dst, ddeps = (out, odep) if last else (hs, hdep)
                emit_ffn(nc, kb, l, S, src, hdep, dst, ddeps, cdep, locals(), fin)

        for d in odep:
            kb._wait(SP, d.w)
    return nc


def bcast_load(kb, dst, src_row, n, cdep):
    kb.dma(kb.sp, dst.t[:, 0:n], src_row.partition_broadcast(128), writes=[dst.d], owner=cdep)


def norm_chain(nc, kb, G, src_tile_ap, slot, gT, hdep_r):
    PE, ACT, DVE, POOL, SP = kb.pe, kb.act, kb.dve, kb.pool, kb.sp
    hA, nb, ss = G["hA"][slot], G["nb"][slot], G["ss"][slot]
    kb.dma(SP, hA.t[:], src_tile_ap, reads=[hdep_r], writes=[hA.d])
    kb.op(ACT, lambda: nc.scalar.activation(out=nb.t[:], in_=hA.t[:], func=AF.Square,
                                            accum_out=ss.t[:, 0:1]),
          reads=[hA.d], writes=[nb.d, ss.d])
    kb.op(ACT, lambda: nc.scalar.activation(out=ss.t[:, 1:2], in_=ss.t[:, 0:1], func=AF.Sqrt,
                                            bias=G["epsc"].t[:, 0:1], scale=1.0 / D),
          reads=[ss.d, G["epsc"].d], writes=[ss.d])
    kb.op(DVE, lambda: nc.vector.reciprocal(out=ss.t[:, 2:3], in_=ss.t[:, 1:2]),
          reads=[ss.d], writes=[ss.d])
    kb.op(DVE, lambda: nc.vector.scalar_tensor_tensor(
        out=nb.t[:], in0=hA.t[:], scalar=ss.t[:, 2:3], in1=gT.t[:], op0=ALU.mult, op1=ALU.mult),
        reads=[hA.d, ss.d, gT.d], writes=[nb.d])


def norm_T(nc, kb, G, slot, PS_T, nT, col0):
    PE, DVE = kb.pe, kb.dve
    nb, ident = G["nb"][slot], G["ident"]
    for kc in range(8):
        kb.op(PE, lambda kc=kc: nc.tensor.transpose(
            PS_T.t[:, kc * 128:(kc + 1) * 128], nb.t[:, kc * 128:(kc + 1) * 128], ident.t[:]),
            reads=[nb.d, ident.d], writes=[PS_T.d], signal=(kc == 7))
    kb.op(DVE, lambda: nc.vector.tensor_copy(
        out=nT.t[:, :, col0:col0 + 128], in_=PS_T.t[:].rearrange("p (k c) -> p k c", k=8)),
        reads=[PS_T.d], writes=[nT.d])


def emit_mixer(nc, kb, l, S, src, hdep, dst, ddeps, cdep, env):
    PE, ACT, DVE, POOL, SP = kb.pe, kb.act, kb.dve, kb.pool, kb.sp
    NT, NSB = S // 128, S // 512
    ident, maskT, cosT, sinT = env["ident"], env["maskT"], env["cosT"], env["sinT"]
    lambda_init = 0.8 - 0.6 * math.exp(-0.3 * l)
    with contextlib.ExitStack() as st:
        kb.phase_id = getattr(kb, "phase_id", 0) + 1
        pfx = "p%d_" % kb.phase_id

        def sb(name, shape, dt):
            return T(st.enter_context(nc.sbuf_tensor(pfx + name, list(shape), dt)), pfx + name)

        def ps(name, shape, dt):
            r = T(st.enter_context(nc.psum_tensor(pfx + name, list(shape), dt)), pfx + name)
            r.d.excl = True
            return r

        wi = sb("wi", [128, 8, DIN], BF16)
        wo = sb("wo", [128, 8, D], BF16)
        for kc in range(8):
            kb.dma(POOL, wi.t[:, kc, :], env["w_in"][l, kc * 128:(kc + 1) * 128, :], writes=[wi.d], waw=False)
        for kc in range(8):
            kb.dma(POOL, wo.t[:, kc, :], env["w_out"][l, kc * 128:(kc + 1) * 128, :], writes=[wo.d], waw=False)
        kT = sb("kT", [128, NH, S], BF16)
        vaug = sb("vaug", [128, NT, NH, VW], BF16)
        nT = sb("nT", [128, 8, 512], BF16)
        uT = sb("uT", [128, 4, 512], F32)
        vg = sb("vg", [128, 4, 512], BF16)
        qT = sb("qT", [128, NH, 512], BF16)
        mixT = nT
        otok = sb("otok", [128, 4, 512], BF16)
        G = dict(ident=ident)
        G["hA"] = [sb("hA%d" % i, [128, D], F32) for i in range(2)]
        G["nb"] = [sb("nb%d" % i, [128, D], BF16) for i in range(2)]
        G["ss"] = [sb("ss%d" % i, [128, 4], F32) for i in range(2)]
        hE = [sb("hE%d" % i, [128, D], F32) for i in range(2)]
        vtmp = [sb("vtmp%d" % i, [128, 512], F32) for i in range(1)] * 2
        vst = [sb("vst%d" % i, [128, 16], F32) for i in range(2)]
        qkb = [sb("qkb%d" % i, [128, 2, 512], BF16) for i in range(2)]
        rtmp = [sb("rtmp%d" % i, [128, 4, 16, 8], F32) for i in range(1)] * 2
        PT = [sb("PT%d" % i, [128, 2, 512], BF16) for i in range(3)]
        oacc = sb("oacc", [128, 8, 129], F32)
        o1 = sb("o1", [128, 4, 128], F32)
        o2 = sb("o2", [128, 4, 128], F32)
        ost = sb("ost", [128, 24], F32)
        sgt = [sb("sgt%d" % i, [128, 512], F32) for i in range(1)] * 2
        gA = sb("gA", [128, D], F32)
        vgain = sb("vgain", [128, 512], F32)
        subg = sb("subg", [128, 128], F32)
        bb = sb("bb", [128, 512], F32)
        lams = sb("lams", [128, 8], F32)
        epsc = sb("epsc", [128, 1], F32)
        G["epsc"] = epsc
        wT = sb("wT", [128, 4, 128], BF16)
        junk = sb("junk", [128, 128], F32)

        PS_X = [ps("PS_X%d" % i, [128, 1024], F32) for i in range(2)]
        PS_O = [ps("PS_O%d" % i, [128, 512], F32) for i in range(3)]
        PS_T = ps("PS_T", [128, 1024], BF16)
        G["PS_T"] = PS_T
        XD = [[Dep("x%d_%d" % (i, j)) for j in range(2)] for i in range(2)]
        for dd_ in XD[0] + XD[1]:
            dd_.excl = True

        PTD = [[Dep("ptd%d_%d" % (i, j)) for j in range(2)] for i in range(3)]
        lamt = T(o2.t, "lamt_alias")
        lamt.t = o2.t[:].rearrange("p a (b c) -> p (a b) c", c=64)[:, 0:4, :]
        wnat = o1
        wnb = T(PT[0].t[:].rearrange("p m q -> p (m q)")[:, 0:512].rearrange("p (g c) -> p g c", g=4), "wnb_alias")
        wnb.d = PTD[0][0]
        lamt.d = o2.d
        kb.op(DVE, lambda: nc.vector.memset(epsc.t[:], EPS), writes=[epsc.d])
        bcast_load(kb, gA, env["attn_norm"][l], D, cdep)
        bcast_load(kb, vgain, env["sgu_v_norm"][l], 512, cdep)
        bcast_load(kb, subg, env["subln"][l], 128, cdep)
        bcast_load(kb, bb, env["sgu_b"][l].rearrange("g p -> (g p)"), 512, cdep)
        for i, nm in enumerate(("lq1", "lk1", "lq2", "lk2")):
            kb.dma(SP, lamt.t[:, i, :], env[nm][l].partition_broadcast(128), writes=[lamt.d], owner=cdep,
                   waw=False)
        kb.dma(SP, wnat.t[:], env["sgu_w"][l].rearrange("g p q -> p g q"), writes=[wnat.d], owner=cdep)
        kb.op(DVE, lambda: nc.vector.tensor_scalar(out=subg.t[:], in0=subg.t[:], scalar1=1.0 - lambda_init,
                                                   scalar2=None, op0=ALU.mult),
              reads=[subg.d], writes=[subg.d])
        kb.op(DVE, lambda: nc.vector.tensor_tensor(out=lamt.t[:, 0, :], in0=lamt.t[:, 0, :], in1=lamt.t[:, 1, :],
                                                   op=ALU.mult), reads=[lamt.d], writes=[lamt.d])
        kb.op(DVE, lambda: nc.vector.tensor_tensor(out=lamt.t[:, 2, :], in0=lamt.t[:, 2, :], in1=lamt.t[:, 3, :],
                                                   op=ALU.mult), reads=[lamt.d], writes=[lamt.d])
        kb.op(DVE, lambda: nc.vector.reduce_sum(out=lams.t[:, 0:1], in_=lamt.t[:, 0, :], axis=AX.X),
              reads=[lamt.d], writes=[lams.d])
        kb.op(DVE, lambda: nc.vector.reduce_sum(out=lams.t[:, 1:2], in_=lamt.t[:, 2, :], axis=AX.X),
              reads=[lamt.d], writes=[lams.d])
        kb.op(ACT, lambda: nc.scalar.activation(out=lams.t[:, 2:4], in_=lams.t[:, 0:2], func=AF.Exp),
              reads=[lams.d], writes=[lams.d])
        kb.op(DVE, lambda: nc.vector.scalar_tensor_tensor(
            out=lams.t[:, 4:5], in0=lams.t[:, 3:4], scalar=-lambda_init, in1=lams.t[:, 2:3],
            op0=ALU.add, op1=ALU.subtract), reads=[lams.d], writes=[lams.d])
        kb.op(POOL, lambda: nc.gpsimd.affine_select(
            out=wnb.t[:], in_=wnat.t[:], pattern=[[0, 4], [-1, 128]], compare_op=ALU.is_ge,
            fill=0.0, base=0, channel_multiplier=1), reads=[wnat.d], writes=[wnb.d])
        for g in range(4):
            kb.op(PE, lambda g=g: nc.tensor.transpose(PS_T.t[:, g * 128:(g + 1) * 128], wnb.t[:, g, :], ident.t[:]),
                  reads=[wnb.d, ident.d], writes=[PS_T.d], signal=(g == 3))
        kb.op(DVE, lambda: nc.vector.tensor_copy(out=wT.t[:], in_=PS_T.t[:, 0:512].rearrange("p (g c) -> p g c", g=4)),
              reads=[PS_T.d], writes=[wT.d])
        kb.op(POOL, lambda: nc.gpsimd.memset(vaug.t[:, :, :, 128:130], 1.0), writes=[vaug.d])

        XH = [(PS_X[i].t[:, j * 512:(j + 1) * 512], XD[i][j]) for i in range(2) for j in range(2)]
        PTH = [(PT[i].t[:, j, :], PTD[i][j]) for i in range(3) for j in range(2)]

        def b1_mm(sbi, tt):
            cols = slice(tt * 128, (tt + 1) * 128)
            targets = [(PS_X[1].t[:, 0:512], 1024, XD[1][0]), (PS_X[1].t[:, 512:1024], 1536, XD[1][1]),
                       (PS_X[0].t[:, 0:512], 512, XD[0][0]), (PS_X[0].t[:, 512:1024], 2048, XD[0][1])]
            for kc in range(8):
                for (o_ap, c0, dd) in targets:
                    kb.op(PE, lambda kc=kc, o_ap=o_ap, c0=c0: nc.tensor.matmul(
                        o_ap, nT.t[:, kc, cols], wi.t[:, kc, c0:c0 + 512], start=(kc == 0), stop=(kc == 7)),
                        reads=[nT.d, wi.d], writes=[dd], signal=(kc == 7))

        def b1_evac(sbi, tt):
            t = sbi * 4 + tt
            qk, rt = qkb[t % 2], rtmp[t % 2]
            psqk = PS_X[1].t[:].rearrange("p (a d) -> p a d", d=64)
            kb.op(ACT, lambda: nc.scalar.activation(out=qk.t[:].rearrange("p a c -> p (a c)"), in_=PS_X[1].t[:],
                                                    func=AF.Copy), reads=XD[1], writes=[qk.d])
            cb_ = cosT.t[:, t:t + 1, :].broadcast_to([128, 16, 8])
            sb_ = sinT.t[:, t:t + 1, :].broadcast_to([128, 16, 8])
            t1, t2 = psqk[:, :, 0:8], psqk[:, :, 8:16]
            for i, (a, b) in enumerate(((t1, cb_), (t2, sb_), (t2, cb_), (t1, sb_))):
                kb.op(DVE, lambda i=i, a=a, b=b: nc.vector.tensor_tensor(out=rt.t[:, i], in0=a, in1=b, op=ALU.mult),
                      reads=XD[1] + [cosT.d, sinT.d], writes=[rt.d])
            qkv = qk.t[:].rearrange("p a (b d) -> p (a b) d", d=64)
            kb.op(DVE, lambda: nc.vector.tensor_tensor(out=qkv[:, :, 0:8], in0=rt.t[:, 0], in1=rt.t[:, 1],
                                                       op=ALU.subtract), reads=[rt.d], writes=[qk.d])
            kb.op(DVE, lambda: nc.vector.tensor_tensor(out=qkv[:, :, 8:16], in0=rt.t[:, 2], in1=rt.t[:, 3],
                                                       op=ALU.add), reads=[rt.d], writes=[qk.d])
            vt, vs = vtmp[t % 2], vst[t % 2]
            kb.op(ACT, lambda: nc.scalar.activation(out=vt.t[:], in_=PS_X[0].t[:, 0:512], func=AF.Gelu_apprx_tanh),
                  reads=[XD[0][0]], writes=[vt.d])
            kb.op(ACT, lambda: nc.scalar.activation(
                out=vaug.t[:, t, :, 0:128], in_=PS_X[0].t[:, 512:1024].rearrange("p (h d) -> p h d", h=4),
                func=AF.Copy), reads=[XD[0][1]], writes=[vaug.d])
            for g in range(4):
                kb.op(ACT, lambda g=g: nc.scalar.activation(
                    out=junk.t[:], in_=vt.t[:, g * 128:(g + 1) * 128], func=AF.Square,
                    accum_out=vs.t[:, g:g + 1]), reads=[vt.d], writes=[junk.d, vs.d])
            kb.op(ACT, lambda: nc.scalar.activation(out=vs.t[:, 4:8], in_=vs.t[:, 0:4], func=AF.Sqrt,
                                                    bias=epsc.t[:, 0:1], scale=1.0 / 128),
                  reads=[vs.d, epsc.d], writes=[vs.d])
            kb.op(DVE, lambda: nc.vector.reciprocal(out=vs.t[:, 8:12], in_=vs.t[:, 4:8]), reads=[vs.d], writes=[vs.d])
            for g in range(4):
                kb.op(DVE, lambda g=g: nc.vector.scalar_tensor_tensor(
                    out=vg.t[:, tt, g * 128:(g + 1) * 128], in0=vt.t[:, g * 128:(g + 1) * 128],
                    scalar=vs.t[:, 8 + g:9 + g], in1=vgain.t[:, g * 128:(g + 1) * 128],
                    op0=ALU.mult, op1=ALU.mult), reads=[vt.d, vs.d, vgain.d], writes=[vg.d])

        def tqk(sbi, tt):
            t = sbi * 4 + tt
            cols = slice(tt * 128, (tt + 1) * 128)
            qk = qkb[t % 2]
            for a in range(2):
                for h in range(4):
                    i = a * 4 + h
                    kb.op(PE, lambda a=a, h=h, i=i: nc.tensor.transpose(
                        PS_T.t[:, i * 128:(i + 1) * 128], qk.t[:, a, h * 128:(h + 1) * 128], ident.t[:]),
                        reads=[qk.d, ident.d], writes=[PS_T.d], signal=(i == 7))
            kb.op(DVE, lambda: nc.vector.tensor_copy(
                out=qT.t[:, :, cols], in_=PS_T.t[:, 0:512].rearrange("p (h c) -> p h c", h=4)),
                reads=[PS_T.d], writes=[qT.d])
            kb.op(DVE, lambda: nc.vector.tensor_copy(
                out=kT.t[:, :, t * 128:(t + 1) * 128], in_=PS_T.t[:, 512:1024].rearrange("p (h c) -> p h c", h=4)),
                reads=[PS_T.d], writes=[kT.d])

        def u_grp(g):
            po = PS_O[g % 2]
            for kc in range(8):
                kb.op(PE, lambda kc=kc: nc.tensor.matmul(
                    po.t[:], wi.t[:, kc, g * 128:(g + 1) * 128], nT.t[:, kc, :], start=(kc == 0), stop=(kc == 7)),
                    reads=[nT.d, wi.d], writes=[po.d], signal=(kc == 7))
            kb.op(ACT, lambda: nc.scalar.activation(out=uT.t[:, g, :], in_=po.t[:], func=AF.Gelu_apprx_tanh),
                  reads=[po.d], writes=[uT.d])

        def sgu(tt):
            cols = slice(tt * 128, (tt + 1) * 128)
            po = PS_O[2]
            for g in range(4):
                kb.op(PE, lambda g=g: nc.tensor.matmul(
                    po.t[:, g * 128:(g + 1) * 128], vg.t[:, tt, g * 128:(g + 1) * 128], wT.t[:, g, :],
                    start=True, stop=True), reads=[vg.d, wT.d], writes=[po.d], signal=(g == 3))
            sg = sgt[tt % 2]
            kb.op(DVE, lambda: nc.vector.tensor_tensor(out=sg.t[:], in0=po.t[:], in1=bb.t[:], op=ALU.add),
                  reads=[po.d, bb.d], writes=[sg.d])
            kb.op(POOL, lambda: nc.gpsimd.tensor_tensor(
                out=mixT.t[:, 0:4, cols], in0=sg.t[:].rearrange("p (g c) -> p g c", g=4), in1=uT.t[:, :, cols],
                op=ALU.mult), reads=[sg.d, uT.d], writes=[mixT.d])

        for sbi in range(NSB):
            t0 = sbi * 4

            def chain(tt):
                t = t0 + tt
                norm_chain(nc, kb, G, src[t * 128:(t + 1) * 128, :], t % 2, gA, hdep[t])

            def ntr(tt):
                norm_T(nc, kb, G, (t0 + tt) % 2, PS_T, nT, tt * 128)

            chain(0); chain(1); ntr(0); chain(2); ntr(1); chain(3); ntr(2); ntr(3)
            b1_mm(sbi, 0); b1_evac(sbi, 0); u_grp(0); u_grp(1)
            b1_mm(sbi, 1); tqk(sbi, 0); b1_evac(sbi, 1); u_grp(2)
            b1_mm(sbi, 2); tqk(sbi, 1); b1_evac(sbi, 2); u_grp(3)
            b1_mm(sbi, 3); tqk(sbi, 2); b1_evac(sbi, 3); sgu(0); sgu(1); sgu(2)
            tqk(sbi, 3); sgu(3)
            nfull = sbi * 4
            items = [(h, c) for h in range(NH) for c in range(nfull + 4)]

            def emit_qk(idx):
                h, c = items[idx]
                X = PS_X[idx % 2]
                q0 = max(0, c - nfull) * 128
                for m in range(2):
                    kb.op(PE, lambda m=m: nc.tensor.matmul(
                        X.t[:, m * 512 + q0:(m + 1) * 512], kT.t[m * 64:(m + 1) * 64, h, c * 128:(c + 1) * 128],
                        qT.t[m * 64:(m + 1) * 64, h, q0:512], start=True, stop=True),
                        reads=[kT.d, qT.d], writes=XD[idx % 2], signal=(m == 1))

            emit_qk(0)
            for idx, (h, c) in enumerate(items):
                if idx + 1 < len(items):
                    emit_qk(idx + 1)
                X = PS_X[idx % 2]
                P = PT[idx % 3]
                pd = PTD[idx % 3]
                j = max(0, c - nfull)
                q0 = j * 128
                kb.op(ACT, lambda: nc.scalar.activation(
                    out=P.t[:, :, q0:512], in_=X.t[:].rearrange("p (m q) -> p m q", m=2)[:, :, q0:512],
                    func=AF.Exp, scale=0.125), reads=XD[idx % 2], writes=pd)
                if c >= nfull:
                    kb.op(POOL, lambda: nc.gpsimd.tensor_tensor(
                        out=P.t[:, :, q0:q0 + 128], in0=P.t[:, :, q0:q0 + 128],
                        in1=maskT.t[:].unsqueeze(1).broadcast_to([128, 2, 128]), op=ALU.mult),
                        reads=pd + [maskT.d], writes=pd)
                for qb in range(j, 4):
                    for m in range(2):
                        a = qb * 2 + m
                        bank, slot = a // 3, a % 3
                        kb.op(PE, lambda qb=qb, m=m, bank=bank, slot=slot: nc.tensor.matmul(
                            PS_O[bank].t[:, slot * 129:(slot + 1) * 129], P.t[:, m, qb * 128:(qb + 1) * 128],
                            vaug.t[:, c, h, 0:129], start=(c == 0 and slot == 0), stop=(c == nfull + qb),
                            skip_group_check=True),
                            reads=pd + [vaug.d], writes=[PS_O[bank].d], signal=(qb == 3 and m == 1))
                m = 1
                if c == nfull + 3 and m == 1:
                    for bank in range(3):
                        n = 3 if bank < 2 else 2
                        kb.op(DVE, lambda bank=bank, n=n: nc.vector.tensor_copy(
                            out=oacc.t[:, bank * 3:bank * 3 + n, :].rearrange("p a c -> p (a c)"),
                            in_=PS_O[bank].t[:, 0:n * 129]), reads=[PS_O[bank].d], writes=[oacc.d])
                    ov = oacc.t[:].rearrange("p (q m) c -> p q m c", m=2)
                    kb.op(DVE, lambda: nc.vector.reciprocal(out=ost.t[:, 0:8], in_=oacc.t[:, :, 128]),
                          reads=[oacc.d], writes=[ost.d])
                    osv = ost.t[:, 0:8].rearrange("p (q m) -> p q m", m=2)
                    kb.op(DVE, lambda: nc.vector.tensor_scalar(out=ost.t[:, 8:12], in0=osv[:, :, 1], scalar1=lams.t[:, 4:5],
                                                               scalar2=None, op0=ALU.mult),
                          reads=[ost.d, lams.d], writes=[ost.d])
                    kb.op(DVE, lambda: nc.vector.tensor_tensor(
                        out=o1.t[:], in0=ov[:, :, 0, 0:128], in1=osv[:, :, 0:1].broadcast_to([128, 4, 128]), op=ALU.mult),
                        reads=[oacc.d, ost.d], writes=[o1.d])
                    kb.op(DVE, lambda: nc.vector.tensor_tensor(
                        out=o2.t[:], in0=ov[:, :, 1, 0:128], in1=ost.t[:, 8:12].unsqueeze(2).broadcast_to([128, 4, 128]),
                        op=ALU.mult), reads=[oacc.d, ost.d], writes=[o2.d])
                    kb.op(DVE, lambda: nc.vector.tensor_tensor(out=o1.t[:], in0=o1.t[:], in1=o2.t[:], op=ALU.add),
                          reads=[o1.d, o2.d], writes=[o1.d])
                    kb.op(DVE, lambda: nc.vector.tensor_tensor(out=o2.t[:], in0=o1.t[:], in1=o1.t[:], op=ALU.mult),
                          reads=[o1.d], writes=[o2.d])
                    kb.op(DVE, lambda: nc.vector.reduce_sum(out=ost.t[:, 12:16], in_=o2.t[:], axis=AX.X),
                          reads=[o2.d], writes=[ost.d])
                    kb.op(ACT, lambda: nc.scalar.activation(out=ost.t[:, 16:20], in_=ost.t[:, 12:16], func=AF.Sqrt,
                                                            bias=epsc.t[:, 0:1], scale=1.0 / 128),
                          reads=[ost.d, epsc.d], writes=[ost.d])
                    kb.op(DVE, lambda: nc.vector.reciprocal(out=ost.t[:, 20:24], in_=ost.t[:, 16:20]),
                          reads=[ost.d], writes=[ost.d])
                    kb.op(DVE, lambda: nc.vector.tensor_tensor(
                        out=o1.t[:], in0=o1.t[:], in1=ost.t[:, 20:24].unsqueeze(2).broadcast_to([128, 4, 128]), op=ALU.mult),
                        reads=[o1.d, ost.d], writes=[o1.d])
                    kb.op(DVE, lambda h=h: nc.vector.tensor_tensor(
                        out=otok.t[:, :, h * 128:(h + 1) * 128], in0=o1.t[:],
                        in1=subg.t[:].unsqueeze(1).broadcast_to([128, 4, 128]), op=ALU.mult),
                        reads=[o1.d, subg.d], writes=[otok.d])
            for tt in range(4):
                cols = slice(tt * 128, (tt + 1) * 128)
                for h in range(4):
                    kb.op(PE, lambda h=h: nc.tensor.transpose(
                        PS_T.t[:, h * 128:(h + 1) * 128], otok.t[:, tt, h * 128:(h + 1) * 128], ident.t[:]),
                        reads=[otok.d, ident.d], writes=[PS_T.d], signal=(h == 3))
                kb.op(DVE, lambda: nc.vector.tensor_copy(
                    out=mixT.t[:, 4:8, cols], in_=PS_T.t[:, 0:512].rearrange("p (h c) -> p h c", h=4)),
                    reads=[PS_T.d], writes=[mixT.d])
            for tt in range(4):
                t = t0 + tt
                cols = slice(tt * 128, (tt + 1) * 128)
                he = hE[t % 2]
                kb.dma(SP, he.t[:], src[t * 128:(t + 1) * 128, :], reads=[hdep[t]], writes=[he.d])
                for cb in range(2):
                    po = PS_O[cb]
                    for kc in range(8):
                        kb.op(PE, lambda kc=kc, cb=cb, po=po: nc.tensor.matmul(
                            po.t[:], mixT.t[:, kc, cols], wo.t[:, kc, cb * 512:(cb + 1) * 512],
                            start=(kc == 0), stop=(kc == 7)), reads=[mixT.d, wo.d], writes=[po.d], signal=(kc == 7))
                    kb.op(DVE, lambda cb=cb, po=po: nc.vector.tensor_tensor(
                        out=he.t[:, cb * 512:(cb + 1) * 512], in0=po.t[:], in1=he.t[:, cb * 512:(cb + 1) * 512], op=ALU.add),
                        reads=[po.d, he.d], writes=[he.d])
                kb.dma(SP, dst[t * 128:(t + 1) * 128, :], he.t[:], reads=[he.d], writes=[ddeps[t]], owner=he.d)
        phase_barrier(nc, kb)


def phase_barrier(nc, kb):
    engs = [kb.pe, kb.act, kb.dve, kb.pool, kb.sp]
    tiks = [Tik(E.sem, E.cnt) for E in engs if E.cnt > 0]
    tiks += [Tik(d.dsem, d.dcnt) for d in kb.owners if d.dcnt > 0]
    for E in engs:
        assert E.pending.val is None
        for t in tiks:
            kb._wait(E, t)


def emit_ffn(nc, kb, l, S, src, hdep, dst, ddeps, cdep, env, fin):
    PE, ACT, DVE, POOL, SP = kb.pe, kb.act, kb.dve, kb.pool, kb.sp
    NT, NSB = S // 128, S // 512
    ident = env["ident"]
    with contextlib.ExitStack() as st:
        kb.phase_id = getattr(kb, "phase_id", 0) + 1
        pfx = "p%d_" % kb.phase_id

        def sb(name, shape, dt):
            return T(st.enter_context(nc.sbuf_tensor(pfx + name, list(shape), dt)), pfx + name)

        def ps(name, shape, dt):
            r = T(st.enter_context(nc.psum_tensor(pfx + name, list(shape), dt)), pfx + name)
            r.d.excl = True
            return r

        wu = sb("wu", [128, 8, 2 * DFF], BF16)
        wd = sb("wd", [128, NCG, D], BF16)
        CGC = 6
        nchunk = (NCG + CGC - 1) // CGC
        wud = [Dep("wud%d_%d" % (kb.phase_id, q)) for q in range(nchunk)]
        for q in range(nchunk):
            g0, g1 = q * CGC, min(NCG, (q + 1) * CGC)
            for kc in range(8):
                for hh in range(2):
                    c0, c1 = hh * DFF + g0 * 128, hh * DFF + g1 * 128
                    kb.dma(POOL, wu.t[:, kc, c0:c1], env["w_up"][l, kc * 128:(kc + 1) * 128, c0:c1],
                           writes=[wud[q]], waw=False)
        for cg in range(NCG):
            kb.dma(POOL, wd.t[:, cg, :], env["w_down"][l, cg * 128:(cg + 1) * 128, :], writes=[wd.d], waw=False)
        nT = sb("nT", [128, 8, 512], BF16)
        actT = sb("actT", [128, NCG, 512], BF16)
        G = dict(ident=ident)
        G["hA"] = [sb("hA%d" % i, [128, D], F32) for i in range(2)]
        G["nb"] = [sb("nb%d" % i, [128, D], BF16) for i in range(2)]
        G["ss"] = [sb("ss%d" % i, [128, 4], F32) for i in range(2)]
        hE = [sb("hE%d" % i, [128, D], F32) for i in range(2)]
        fs = [sb("fs%d" % i, [128, 4], F32) for i in range(2)]
        Ag = [sb("Ag%d" % i, [128, 512], F32) for i in range(2)]
        Av = [sb("Av%d" % i, [128, 512], F32) for i in range(2)]
        tails = sb("tails", [128, 2, 2 * NCG, 2], F32)
        tdep = [Dep("tail0"), Dep("tail1")]
        cw = sb("cw", [128, 3, 2 * NCG], F32)
        cbias = sb("cbias", [128, 2 * NCG], F32)
        gF = sb("gF", [128, D], F32)
        epsc = sb("epsc", [128, 1], F32)
        G["epsc"] = epsc
        PS_G = [ps("PS_G%d" % i, [128, 512], F32) for i in range(2)]
        PS_V = [ps("PS_V%d" % i, [128, 512], F32) for i in range(2)]
        PS_D = [ps("PS_D%d" % i, [128, 512], F32) for i in range(2)]
        PS_T = [ps("PS_T%d" % i, [128, 1024], BF16) for i in range(2)]

        kb.op(DVE, lambda: nc.vector.memset(epsc.t[:], EPS), writes=[epsc.d])
        bcast_load(kb, gF, env["ffn_norm"][l], D, cdep)
        if fin:
            gfin = sb("gfin", [128, D], F32)
            fjunk = sb("fjunk", [128, D], BF16)
            bcast_load(kb, gfin, env["final_norm"], D, cdep)
        for j in range(3):
            kb.dma(SP, cw.t[:, j, :], env["conv_w"][l, j].rearrange("(g p) -> p g", p=128), writes=[cw.d],
                   waw=False, allow_slow_non_contiguous=True)
        kb.dma(SP, cbias.t[:], env["conv_b"][l].rearrange("(g p) -> p g", p=128), writes=[cbias.d],
               allow_slow_non_contiguous=True)

        def chain(t):
            norm_chain(nc, kb, G, src[t * 128:(t + 1) * 128, :], t % 2, gF, hdep[t])

        def ntr(t):
            norm_T(nc, kb, G, t % 2, PS_T[t % 2], nT, (t % 4) * 128)

        chain(0); chain(1); ntr(0); chain(2); ntr(1); chain(3); ntr(2); ntr(3)
        for sbi in range(NSB):
            t0 = sbi * 4
            for cg in range(NCG):
                pg, pv = PS_G[cg % 2], PS_V[cg % 2]
                ag, av = Ag[cg % 2], Av[cg % 2]
                for (P, c0) in ((pg, cg * 128), (pv, DFF + cg * 128)):
                    for kc in range(8):
                        kb.op(PE, lambda P=P, c0=c0, kc=kc: nc.tensor.matmul(
                            P.t[:], wu.t[:, kc, c0:c0 + 128], nT.t[:, kc, :], start=(kc == 0), stop=(kc == 7)),
                            reads=[wud[cg // CGC], nT.d], writes=[P.d], signal=(kc == 7))
                for (P, A, gi) in ((pg, ag, cg), (pv, av, NCG + cg)):
                    w0, w1, w2 = (cw.t[:, j, gi:gi + 1] for j in range(3))
                    kb.op(ACT, lambda P=P, A=A, gi=gi, w2=w2: nc.scalar.activation(
                        out=A.t[:], in_=P.t[:], func=AF.Identity, bias=cbias.t[:, gi:gi + 1], scale=w2),
                        reads=[P.d, cw.d, cbias.d], writes=[A.d])
                    kb.op(DVE, lambda P=P, A=A, w1=w1: nc.vector.scalar_tensor_tensor(
                        out=A.t[:, 1:512], in0=P.t[:, 0:511], scalar=w1, in1=A.t[:, 1:512], op0=ALU.mult, op1=ALU.add),
                        reads=[P.d, A.d, cw.d], writes=[A.d])
                    kb.op(DVE, lambda P=P, A=A, w0=w0: nc.vector.scalar_tensor_tensor(
                        out=A.t[:, 2:512], in0=P.t[:, 0:510], scalar=w0, in1=A.t[:, 2:512], op0=ALU.mult, op1=ALU.add),
                        reads=[P.d, A.d, cw.d], writes=[A.d])
                    if sbi > 0:
                        tl = tails.t[:, (sbi - 1) % 2, gi, :]
                        td = tdep[(sbi - 1) % 2]
                        kb.op(DVE, lambda A=A, tl=tl, w1=w1: nc.vector.scalar_tensor_tensor(
                            out=A.t[:, 0:1], in0=tl[:, 1:2], scalar=w1, in1=A.t[:, 0:1], op0=ALU.mult, op1=ALU.add),
                            reads=[td, A.d, cw.d], writes=[A.d])
                        kb.op(DVE, lambda A=A, tl=tl, w0=w0: nc.vector.scalar_tensor_tensor(
                            out=A.t[:, 0:2], in0=tl[:, 0:2], scalar=w0, in1=A.t[:, 0:2], op0=ALU.mult, op1=ALU.add),
                            reads=[td, A.d, cw.d], writes=[A.d])
                    if sbi < NSB - 1:
                        kb.op(ACT, lambda P=P, gi=gi: nc.scalar.activation(
                            out=tails.t[:, sbi % 2, gi, :], in_=P.t[:, 510:512], func=AF.Copy),
                            reads=[P.d], writes=[tdep[sbi % 2]])
                kb.op(ACT, lambda ag=ag: nc.scalar.activation(out=ag.t[:], in_=ag.t[:], func=AF.Silu),
                      reads=[ag.d], writes=[ag.d])
                kb.op(POOL, lambda ag=ag, av=av, cg=cg: nc.gpsimd.tensor_tensor(
                    out=actT.t[:, cg, :], in0=ag.t[:], in1=av.t[:], op=ALU.mult),
                    reads=[ag.d, av.d], writes=[actT.d])
            nxt = (sbi + 1) * 4 if sbi + 1 < NSB else None
            if nxt is not None:
                chain(nxt); chain(nxt + 1)
            for tt in range(4):
                t = t0 + tt
                cols = slice(tt * 128, (tt + 1) * 128)
                he = hE[t % 2]
                if nxt is not None:
                    if tt >= 1:
                        ntr(nxt + tt - 1)
                        if tt + 1 < 4:
                            chain(nxt + tt + 1)
                kb.dma(SP, he.t[:], src[t * 128:(t + 1) * 128, :], reads=[hdep[t]], writes=[he.d])
                for cb in range(2):
                    po = PS_D[cb]
                    for cg in range(NCG):
                        kb.op(PE, lambda cg=cg, cb=cb, po=po: nc.tensor.matmul(
                            po.t[:], actT.t[:, cg, cols], wd.t[:, cg, cb * 512:(cb + 1) * 512],
                            start=(cg == 0), stop=(cg == NCG - 1)), reads=[actT.d, wd.d], writes=[po.d],
                            signal=(cg == NCG - 1))
                    kb.op(DVE, lambda cb=cb, po=po: nc.vector.tensor_tensor(
                        out=he.t[:, cb * 512:(cb + 1) * 512], in0=po.t[:], in1=he.t[:, cb * 512:(cb + 1) * 512], op=ALU.add),
                        reads=[po.d, he.d], writes=[he.d])
                if fin:
                    f = fs[t % 2]
                    nbj = fjunk
                    kb.op(ACT, lambda: nc.scalar.activation(out=nbj.t[:], in_=he.t[:], func=AF.Square,
                                                            accum_out=f.t[:, 0:1]),
                          reads=[he.d], writes=[nbj.d, f.d])
                    kb.op(ACT, lambda: nc.scalar.activation(out=f.t[:, 1:2], in_=f.t[:, 0:1], func=AF.Sqrt,
                                                            bias=epsc.t[:, 0:1], scale=1.0 / D),
                          reads=[f.d, epsc.d], writes=[f.d])
                    kb.op(DVE, lambda: nc.vector.reciprocal(out=f.t[:, 2:3], in_=f.t[:, 1:2]), reads=[f.d], writes=[f.d])
                    kb.op(DVE, lambda: nc.vector.scalar_tensor_tensor(
                        out=he.t[:], in0=he.t[:], scalar=f.t[:, 2:3], in1=gfin.t[:], op0=ALU.mult, op1=ALU.mult),
                        reads=[he.d, f.d, gfin.d], writes=[he.d])
                kb.dma(SP, dst[t * 128:(t + 1) * 128, :], he.t[:], reads=[he.d], writes=[ddeps[t]], owner=he.d)
            if nxt is not None:
                ntr(nxt + 3)
        phase_barrier(nc, kb)


_CACHE = {}


def kernel(**inputs):
    S = 4096
    nb = 8
    if S not in _CACHE:
        _CACHE[S] = build_program(S, 4)
    nc = _CACHE[S]
    names = ["attn_norm", "w_in", "sgu_v_norm", "sgu_w_spatial", "sgu_b_spatial", "lambda_q1", "lambda_k1",
             "lambda_q2", "lambda_k2", "subln_gain", "w_out", "ffn_norm", "w_up", "conv_w", "conv_b", "w_down",
             "final_norm"]
    shared = {n: np.ascontiguousarray(np.asarray(inputs[n], dtype=np.float32)) for n in names}
    x = np.asarray(inputs["x"], dtype=np.float32)
    pos = np.asarray(inputs["positions"], dtype=np.int32)
    in_maps = []
    for b in range(nb):
        m = dict(shared)
        m["x"] = np.ascontiguousarray(x[b])
        m["positions"] = np.ascontiguousarray(pos[b])
        in_maps.append(m)
    res = run_bass_kernel_spmd(nc, in_maps, core_ids=list(range(nb)))
    return np.stack([np.asarray(r["out"], dtype=np.float32) for r in res.results], axis=0)
```

```python
import contextlib
import math
import numpy as np
import concourse.bass as bass
import concourse.mybir as mybir
from concourse.bass_utils import run_bass_kernel_spmd

F32 = mybir.dt.float32
BF16 = mybir.dt.bfloat16
I32 = mybir.dt.int32
AF = mybir.ActivationFunctionType
ALU = mybir.AluOpType
AX = mybir.AxisListType

D = 1024
DIN = 2560
DFF = 2816
NCG = DFF // 128
NH = 4
EPS = 1e-6
DEPTH = 2
ROPE_THETA = 500000.0
VW = 130


class Tik:
    __slots__ = ("sem", "val")

    def __init__(self, sem, val):
        self.sem = sem
        self.val = val


class Dep:
    __slots__ = ("name", "w", "r", "dsem", "dcnt", "shared", "excl")

    def __init__(self, name):
        self.name = name
        self.w = None
        self.r = {}
        self.dsem = None
        self.dcnt = 0
        self.shared = None
        self.excl = False


class Eng:
    def __init__(self, kb, name, eng, is_pe=False):
        self.kb = kb
        self.name = name
        self.eng = eng
        self.is_pe = is_pe
        self.sem = kb.new_sem("e_" + name)
        self.cnt = 0
        self.waited = {}
        self.pending = Tik(self.sem, None)


class KB:
    def __init__(self, nc):
        self.nc = nc
        self.stack = contextlib.ExitStack()
        self.nsem = 0
        self.pe = Eng(self, "pe", nc.tensor, True)
        self.act = Eng(self, "act", nc.scalar)
        self.dve = Eng(self, "dve", nc.vector)
        self.pool = Eng(self, "pool", nc.gpsimd)
        self.sp = Eng(self, "sp", nc.sync)
        self.nins = 0
        self.owners = []

    def new_sem(self, name):
        self.nsem += 1
        return self.stack.enter_context(self.nc.semaphore(name))

    def sb(self, name, shape, dt):
        return self.stack.enter_context(self.nc.sbuf_tensor(name, list(shape), dt))

    def ps(self, name, shape, dt):
        return self.stack.enter_context(self.nc.psum_tensor(name, list(shape), dt))

    def _wait(self, E, tik):
        if tik is None:
            return
        if E.is_pe and tik.sem is E.sem:
            return
        assert tik.val is not None, "waiting on an unsignaled instruction"
        if E.waited.get(tik.sem, 0) >= tik.val:
            return
        E.eng.wait_ge(tik.sem, tik.val)
        E.waited[tik.sem] = tik.val

    def _sync(self, E, reads, writes, waw=True):
        for d in reads:
            self._wait(E, d.w)
        for d in writes:
            if waw:
                self._wait(E, d.w)
            for t in list(d.r.values()):
                self._wait(E, t)

    def op(self, E, fn, reads=(), writes=(), signal=True):
        if any(d.excl for d in reads):
            writes = list(writes) + [d for d in reads if d.excl and d not in writes]
            reads = [d for d in reads if not d.excl]
        self._sync(E, reads, writes)
        ins = fn()
        self.nins += 1
        if signal:
            E.cnt += 1
            ins.then_inc(E.sem, 1)
            E.pending.val = E.cnt
            tik = E.pending
            E.pending = Tik(E.sem, None)
        else:
            tik = E.pending
        for d in reads:
            d.r[tik.sem] = tik
        for d in writes:
            d.w = tik
            d.r = {}
        return tik

    def dma(self, Q, out, in_, reads=(), writes=(), owner=None, waw=True, **kw):
        self._sync(Q, reads, writes, waw=waw)
        if owner is None:
            owner = writes[0] if writes else reads[0]
        if owner.dsem is None:
            owner.dsem = self.new_sem("d_" + owner.name)
            self.owners.append(owner)
        owner.dcnt += 16
        Q.eng.dma_start(out=out, in_=in_, **kw).then_inc(owner.dsem, 16)
        self.nins += 1
        if getattr(owner, "shared", None) is not None:
            tik = owner.shared
            tik.sem = owner.dsem
            tik.val = owner.dcnt
        else:
            tik = Tik(owner.dsem, owner.dcnt)
        for d in reads:
            d.r[tik.sem] = tik
        for d in writes:
            d.w = tik
            d.r = {}
        return tik


class T:
    def __init__(self, t, name):
        self.t = t
        self.d = Dep(name)


def build_program(S, nphase=4):
    nc = bass.Bass("TRN2", target_bir_lowering=False)
    NT = S // 128
    NSB = S // 512

    def din(name, shape, dt=F32):
        return nc.dram_tensor(name, list(shape), dt, kind="ExternalInput").ap()

    x = din("x", [S, D])
    positions = din("positions", [S], I32)
    attn_norm = din("attn_norm", [DEPTH, D])
    w_in = din("w_in", [DEPTH, D, DIN])
    sgu_v_norm = din("sgu_v_norm", [DEPTH, 512])
    sgu_w = din("sgu_w_spatial", [DEPTH, 4, 128, 128])
    sgu_b = din("sgu_b_spatial", [DEPTH, 4, 128])
    lq1 = din("lambda_q1", [DEPTH, 64])
    lk1 = din("lambda_k1", [DEPTH, 64])
    lq2 = din("lambda_q2", [DEPTH, 64])
    lk2 = din("lambda_k2", [DEPTH, 64])
    subln = din("subln_gain", [DEPTH, 128])
    w_out = din("w_out", [DEPTH, D, D])
    ffn_norm = din("ffn_norm", [DEPTH, D])
    w_up = din("w_up", [DEPTH, D, 2 * DFF])
    conv_w = din("conv_w", [DEPTH, 3, 2 * DFF])
    conv_b = din("conv_b", [DEPTH, 2 * DFF])
    w_down = din("w_down", [DEPTH, DFF, D])
    final_norm = din("final_norm", [D])
    out = nc.dram_tensor("out", [S, D], F32, kind="ExternalOutput").ap()
    hs = nc.dram_tensor("hs", [S, D], F32).ap()

    kb = KB(nc)
    PE, ACT, DVE, POOL, SP = kb.pe, kb.act, kb.dve, kb.pool, kb.sp
    hdep = [Dep("hs%d" % t) for t in range(NT)]
    odep = [Dep("out%d" % t) for t in range(NT)]
    cdep = Dep("const")
    cdep.shared = Tik(None, 0)

    with kb.stack:
        ident = T(kb.sb("ident", [128, 128], BF16), "ident")
        maskT = T(kb.sb("maskT", [128, 128], BF16), "maskT")
        cosT = T(kb.sb("cosT", [128, NT, 8], F32), "cos")
        sinT = T(kb.sb("sinT", [128, NT, 8], F32), "sin")

        kb.op(POOL, lambda: nc.gpsimd.memset(ident.t[:], 1.0), writes=[ident.d])
        kb.op(POOL, lambda: nc.gpsimd.affine_select(
            out=ident.t[:], in_=ident.t[:], pattern=[[-1, 128]], compare_op=ALU.is_equal,
            fill=0.0, base=0, channel_multiplier=1), reads=[ident.d], writes=[ident.d])
        kb.op(POOL, lambda: nc.gpsimd.memset(maskT.t[:], 1.0), writes=[maskT.d])
        kb.op(POOL, lambda: nc.gpsimd.affine_select(
            out=maskT.t[:], in_=maskT.t[:], pattern=[[1, 128]], compare_op=ALU.is_ge,
            fill=0.0, base=0, channel_multiplier=-1), reads=[maskT.d], writes=[maskT.d])

        with contextlib.ExitStack() as tmp:
            posi = T(tmp.enter_context(nc.sbuf_tensor("posi", [128, NT], I32)), "posi")
            posf = T(tmp.enter_context(nc.sbuf_tensor("posf", [128, NT], F32)), "posf")
            ang = T(tmp.enter_context(nc.sbuf_tensor("ang", [128, NT, 8], F32)), "ang")
            red = T(tmp.enter_context(nc.sbuf_tensor("red", [128, NT, 8], F32)), "red")
            negpi = T(tmp.enter_context(nc.sbuf_tensor("negpi", [128, 1], F32)), "negpi")
            kb.dma(SP, posi.t[:], positions.rearrange("(t p) -> p t", p=128), writes=[posi.d],
                   allow_slow_non_contiguous=True)
            kb.op(DVE, lambda: nc.vector.tensor_copy(out=posf.t[:], in_=posi.t[:]),
                  reads=[posi.d], writes=[posf.d])
            kb.op(DVE, lambda: nc.vector.memset(negpi.t[:], -math.pi), writes=[negpi.d])
            for j in range(8):
                invf = float(np.float32(ROPE_THETA) ** np.float32(-(2.0 * j) / 16.0))
                kb.op(DVE, lambda j=j, invf=invf: nc.vector.tensor_scalar(
                    out=ang.t[:, :, j], in0=posf.t[:], scalar1=invf, scalar2=None, op0=ALU.mult),
                    reads=[posf.d], writes=[ang.d])
            ki = T(tmp.enter_context(nc.sbuf_tensor("ki", [128, NT, 8], I32)), "ki")
            kf = T(tmp.enter_context(nc.sbuf_tensor("kf", [128, NT, 8], F32)), "kf")
            TWO_PI = 2.0 * math.pi
            for (dst, shift) in ((sinT, 0.0), (cosT, 0.5 * math.pi)):
                kb.op(DVE, lambda shift=shift: nc.vector.tensor_scalar(
                    out=red.t[:], in0=ang.t[:], scalar1=shift, scalar2=None, op0=ALU.add),
                    reads=[ang.d], writes=[red.d])
                kb.op(DVE, lambda: nc.vector.tensor_scalar(
                    out=kf.t[:], in0=red.t[:], scalar1=1.0 / TWO_PI, scalar2=None, op0=ALU.mult),
                    reads=[red.d], writes=[kf.d])
                kb.op(DVE, lambda: nc.vector.tensor_copy(out=ki.t[:], in_=kf.t[:]), reads=[kf.d], writes=[ki.d])
                kb.op(DVE, lambda: nc.vector.tensor_copy(out=kf.t[:], in_=ki.t[:]), reads=[ki.d], writes=[kf.d])
                kb.op(DVE, lambda: nc.vector.scalar_tensor_tensor(
                    out=red.t[:], in0=kf.t[:], scalar=-TWO_PI, in1=red.t[:], op0=ALU.mult, op1=ALU.add),
                    reads=[kf.d, red.d], writes=[red.d])
                kb.op(DVE, lambda: nc.vector.tensor_scalar(
                    out=kf.t[:], in0=red.t[:], scalar1=math.pi, scalar2=TWO_PI, op0=ALU.is_gt, op1=ALU.mult),
                    reads=[red.d], writes=[kf.d])
                kb.op(DVE, lambda: nc.vector.tensor_tensor(out=red.t[:], in0=red.t[:], in1=kf.t[:], op=ALU.subtract),
                      reads=[red.d, kf.d], writes=[red.d])
                kb.op(DVE, lambda: nc.vector.tensor_scalar(
                    out=kf.t[:], in0=red.t[:], scalar1=-math.pi, scalar2=TWO_PI, op0=ALU.is_lt, op1=ALU.mult),
                    reads=[red.d], writes=[kf.d])
                kb.op(DVE, lambda: nc.vector.tensor_tensor(out=red.t[:], in0=red.t[:], in1=kf.t[:], op=ALU.add),
                      reads=[red.d, kf.d], writes=[red.d])
                kb.op(DVE, lambda: nc.vector.tensor_scalar(
                    out=red.t[:], in0=red.t[:], scalar1=-3.1415925, scalar2=3.1415925, op0=ALU.max, op1=ALU.min),
                    reads=[red.d], writes=[red.d])
                kb.op(ACT, lambda dst=dst: nc.scalar.activation(out=dst.t[:], in_=red.t[:], func=AF.Sin),
                      reads=[red.d], writes=[dst.d])
            kb.op(DVE, lambda: nc.vector.memset(negpi.t[:], 0.0), reads=[red.d, ang.d, posf.d, posi.d, ki.d, kf.d],
                  writes=[negpi.d, red.d, ang.d, posf.d, posi.d])
        phase_barrier(nc, kb)

        phases = [("mixer", 0), ("ffn", 0), ("mixer", 1), ("ffn", 1)][:nphase]
        for pi_, (kind, l) in enumerate(phases):
            last = pi_ == len(phases) - 1
            src = x if pi_ == 0 else hs
            if kind == "mixer":
                dst, ddeps = (out, odep) if last else (hs, hdep)
                emit_mixer(nc, kb, l, S, src, hdep, dst, ddeps, cdep, locals())
            else:
                fin = last and nphase == 4
                dst, ddeps = (out, odep) if last else (hs, hdep)
                emit_ffn(nc, kb, l, S, src, hdep, dst, ddeps, cdep, locals(), fin)

        for d in odep:
            kb._wait(SP, d.w)
    return nc


def bcast_load(kb, dst, src_row, n, cdep):
    kb.dma(kb.sp, dst.t[:, 0:n], src_row.partition_broadcast(128), writes=[dst.d], owner=cdep)


def norm_chain(nc, kb, G, src_tile_ap, slot, gT, hdep_r):
    PE, ACT, DVE, POOL, SP = kb.pe, kb.act, kb.dve, kb.pool, kb.sp
    hA, nb, ss = G["hA"][slot], G["nb"][slot], G["ss"][slot]
    kb.dma(SP, hA.t[:], src_tile_ap, reads=[hdep_r], writes=[hA.d])
    kb.op(ACT, lambda: nc.scalar.activation(out=nb.t[:], in_=hA.t[:], func=AF.Square,
                                            accum_out=ss.t[:, 0:1]),
          reads=[hA.d], writes=[nb.d, ss.d])
    kb.op(ACT, lambda: nc.scalar.activation(out=ss.t[:, 1:2], in_=ss.t[:, 0:1], func=AF.Sqrt,
                                            bias=G["epsc"].t[:, 0:1], scale=1.0 / D),
          reads=[ss.d, G["epsc"].d], writes=[ss.d])
    kb.op(DVE, lambda: nc.vector.reciprocal(out=ss.t[:, 2:3], in_=ss.t[:, 1:2]),
          reads=[ss.d], writes=[ss.d])
    kb.op(DVE, lambda: nc.vector.scalar_tensor_tensor(
        out=nb.t[:], in0=hA.t[:], scalar=ss.t[:, 2:3], in1=gT.t[:], op0=ALU.mult, op1=ALU.mult),
        reads=[hA.d, ss.d, gT.d], writes=[nb.d])


def norm_T(nc, kb, G, slot, PS_T, nT, col0):
    PE, DVE = kb.pe, kb.dve
    nb, ident = G["nb"][slot], G["ident"]
    for kc in range(8):
        kb.op(PE, lambda kc=kc: nc.tensor.transpose(
            PS_T.t[:, kc * 128:(kc + 1) * 128], nb.t[:, kc * 128:(kc + 1) * 128], ident.t[:]),
            reads=[nb.d, ident.d], writes=[PS_T.d], signal=(kc == 7))
    kb.op(DVE, lambda: nc.vector.tensor_copy(
        out=nT.t[:, :, col0:col0 + 128], in_=PS_T.t[:].rearrange("p (k c) -> p k c", k=8)),
        reads=[PS_T.d], writes=[nT.d])


def emit_mixer(nc, kb, l, S, src, hdep, dst, ddeps, cdep, env):
    PE, ACT, DVE, POOL, SP = kb.pe, kb.act, kb.dve, kb.pool, kb.sp
    NT, NSB = S // 128, S // 512
    ident, maskT, cosT, sinT = env["ident"], env["maskT"], env["cosT"], env["sinT"]
    lambda_init = 0.8 - 0.6 * math.exp(-0.3 * l)
    with contextlib.ExitStack() as st:
        kb.phase_id = getattr(kb, "phase_id", 0) + 1
        pfx = "p%d_" % kb.phase_id

        def sb(name, shape, dt):
            return T(st.enter_context(nc.sbuf_tensor(pfx + name, list(shape), dt)), pfx + name)

        def ps(name, shape, dt):
            r = T(st.enter_context(nc.psum_tensor(pfx + name, list(shape), dt)), pfx + name)
            r.d.excl = True
            return r

        wi = sb("wi", [128, 8, DIN], BF16)
        wo = sb("wo", [128, 8, D], BF16)
        for kc in range(8):
            kb.dma(POOL, wi.t[:, kc, :], env["w_in"][l, kc * 128:(kc + 1) * 128, :], writes=[wi.d], waw=False)
        for kc in range(8):
            kb.dma(POOL, wo.t[:, kc, :], env["w_out"][l, kc * 128:(kc + 1) * 128, :], writes=[wo.d], waw=False)
        kT = sb("kT", [128, NH, S], BF16)
        vaug = sb("vaug", [128, NT, NH, VW], BF16)
        nT = sb("nT", [128, 8, 512], BF16)
        uT = sb("uT", [128, 4, 512], F32)
        vg = sb("vg", [128, 4, 512], BF16)
        qT = sb("qT", [128, NH, 512], BF16)
        mixT = nT
        otok = sb("otok", [128, 4, 512], BF16)
        G = dict(ident=ident)
        G["hA"] = [sb("hA%d" % i, [128, D], F32) for i in range(2)]
        G["nb"] = [sb("nb%d" % i, [128, D], BF16) for i in range(2)]
        G["ss"] = [sb("ss%d" % i, [128, 4], F32) for i in range(2)]
        hE = [sb("hE%d" % i, [128, D], F32) for i in range(2)]
        vtmp = [sb("vtmp%d" % i, [128, 512], F32) for i in range(1)] * 2
        vst = [sb("vst%d" % i, [128, 16], F32) for i in range(2)]
        qkb = [sb("qkb%d" % i, [128, 2, 512], BF16) for i in range(2)]
        rtmp = [sb("rtmp%d" % i, [128, 4, 16, 8], F32) for i in range(1)] * 2
        PT = [sb("PT%d" % i, [128, 2, 512], BF16) for i in range(3)]
        oacc = sb("oacc", [128, 8, 129], F32)
        o1 = sb("o1", [128, 4, 128], F32)
        o2 = sb("o2", [128, 4, 128], F32)
        ost = sb("ost", [128, 24], F32)
        sgt = [sb("sgt%d" % i, [128, 512], F32) for i in range(1)] * 2
        gA = sb("gA", [128, D], F32)
        vgain = sb("vgain", [128, 512], F32)
        subg = sb("subg", [128, 128], F32)
        bb = sb("bb", [128, 512], F32)
        lams = sb("lams", [128, 8], F32)
        epsc = sb("epsc", [128, 1], F32)
        G["epsc"] = epsc
        wT = sb("wT", [128, 4, 128], BF16)
        junk = sb("junk", [128, 128], F32)

        PS_X = [ps("PS_X%d" % i, [128, 1024], F32) for i in range(2)]
        PS_O = [ps("PS_O%d" % i, [128, 512], F32) for i in range(3)]
        PS_T = ps("PS_T", [128, 1024], BF16)
        G["PS_T"] = PS_T
        XD = [[Dep("x%d_%d" % (i, j)) for j in range(2)] for i in range(2)]
        for dd_ in XD[0] + XD[1]:
            dd_.excl = True

        PTD = [[Dep("ptd%d_%d" % (i, j)) for j in range(2)] for i in range(3)]
        lamt = T(o2.t, "lamt_alias")
        lamt.t = o2.t[:].rearrange("p a (b c) -> p (a b) c", c=64)[:, 0:4, :]
        wnat = o1
        wnb = T(PT[0].t[:].rearrange("p m q -> p (m q)")[:, 0:512].rearrange("p (g c) -> p g c", g=4), "wnb_alias")
        wnb.d = PTD[0][0]
        lamt.d = o2.d
        kb.op(DVE, lambda: nc.vector.memset(epsc.t[:], EPS), writes=[epsc.d])
        bcast_load(kb, gA, env["attn_norm"][l], D, cdep)
        bcast_load(kb, vgain, env["sgu_v_norm"][l], 512, cdep)
        bcast_load(kb, subg, env["subln"][l], 128, cdep)
        bcast_load(kb, bb, env["sgu_b"][l].rearrange("g p -> (g p)"), 512, cdep)
        for i, nm in enumerate(("lq1", "lk1", "lq2", "lk2")):
            kb.dma(SP, lamt.t[:, i, :], env[nm][l].partition_broadcast(128), writes=[lamt.d], owner=cdep,
                   waw=False)
        kb.dma(SP, wnat.t[:], env["sgu_w"][l].rearrange("g p q -> p g q"), writes=[wnat.d], owner=cdep)
        kb.op(DVE, lambda: nc.vector.tensor_scalar(out=subg.t[:], in0=subg.t[:], scalar1=1.0 - lambda_init,
                                                   scalar2=None, op0=ALU.mult),
              reads=[subg.d], writes=[subg.d])
        kb.op(DVE, lambda: nc.vector.tensor_tensor(out=lamt.t[:, 0, :], in0=lamt.t[:, 0, :], in1=lamt.t[:, 1, :],
                                                   op=ALU.mult), reads=[lamt.d], writes=[lamt.d])
        kb.op(DVE, lambda: nc.vector.tensor_tensor(out=lamt.t[:, 2, :], in0=lamt.t[:, 2, :], in1=lamt.t[:, 3, :],
                                                   op=ALU.mult), reads=[lamt.d], writes=[lamt.d])
        kb.op(DVE, lambda: nc.vector.reduce_sum(out=lams.t[:, 0:1], in_=lamt.t[:, 0, :], axis=AX.X),
              reads=[lamt.d], writes=[lams.d])
        kb.op(DVE, lambda: nc.vector.reduce_sum(out=lams.t[:, 1:2], in_=lamt.t[:, 2, :], axis=AX.X),
              reads=[lamt.d], writes=[lams.d])
        kb.op(ACT, lambda: nc.scalar.activation(out=lams.t[:, 2:4], in_=lams.t[:, 0:2], func=AF.Exp),
              reads=[lams.d], writes=[lams.d])
        kb.op(DVE, lambda: nc.vector.scalar_tensor_tensor(
            out=lams.t[:, 4:5], in0=lams.t[:, 3:4], scalar=-lambda_init, in1=lams.t[:, 2:3],
            op0=ALU.add, op1=ALU.subtract), reads=[lams.d], writes=[lams.d])
        kb.op(POOL, lambda: nc.gpsimd.affine_select(
            out=wnb.t[:], in_=wnat.t[:], pattern=[[0, 4], [-1, 128]], compare_op=ALU.is_ge,
            fill=0.0, base=0, channel_multiplier=1), reads=[wnat.d], writes=[wnb.d])
        for g in range(4):
            kb.op(PE, lambda g=g: nc.tensor.transpose(PS_T.t[:, g * 128:(g + 1) * 128], wnb.t[:, g, :], ident.t[:]),
                  reads=[wnb.d, ident.d], writes=[PS_T.d], signal=(g == 3))
        kb.op(DVE, lambda: nc.vector.tensor_copy(out=wT.t[:], in_=PS_T.t[:, 0:512].rearrange("p (g c) -> p g c", g=4)),
              reads=[PS_T.d], writes=[wT.d])
        kb.op(POOL, lambda: nc.gpsimd.memset(vaug.t[:, :, :, 128:130], 1.0), writes=[vaug.d])

        XH = [(PS_X[i].t[:, j * 512:(j + 1) * 512], XD[i][j]) for i in range(2) for j in range(2)]
        PTH = [(PT[i].t[:, j, :], PTD[i][j]) for i in range(3) for j in range(2)]

        def b1_mm(sbi, tt):
            cols = slice(tt * 128, (tt + 1) * 128)
            targets = [(PS_X[1].t[:, 0:512], 1024, XD[1][0]), (PS_X[1].t[:, 512:1024], 1536, XD[1][1]),
                       (PS_X[0].t[:, 0:512], 512, XD[0][0]), (PS_X[0].t[:, 512:1024], 2048, XD[0][1])]
            for kc in range(8):
                for (o_ap, c0, dd) in targets:
                    kb.op(PE, lambda kc=kc, o_ap=o_ap, c0=c0: nc.tensor.matmul(
                        o_ap, nT.t[:, kc, cols], wi.t[:, kc, c0:c0 + 512], start=(kc == 0), stop=(kc == 7)),
                        reads=[nT.d, wi.d], writes=[dd], signal=(kc == 7))

        def b1_evac(sbi, tt):
            t = sbi * 4 + tt
            qk, rt = qkb[t % 2], rtmp[t % 2]
            psqk = PS_X[1].t[:].rearrange("p (a d) -> p a d", d=64)
            kb.op(ACT, lambda: nc.scalar.activation(out=qk.t[:].rearrange("p a c -> p (a c)"), in_=PS_X[1].t[:],
                                                    func=AF.Copy), reads=XD[1], writes=[qk.d])
            cb_ = cosT.t[:, t:t + 1, :].broadcast_to([128, 16, 8])
            sb_ = sinT.t[:, t:t + 1, :].broadcast_to([128, 16, 8])
            t1, t2 = psqk[:, :, 0:8], psqk[:, :, 8:16]
            for i, (a, b) in enumerate(((t1, cb_), (t2, sb_), (t2, cb_), (t1, sb_))):
                kb.op(DVE, lambda i=i, a=a, b=b: nc.vector.tensor_tensor(out=rt.t[:, i], in0=a, in1=b, op=ALU.mult),
                      reads=XD[1] + [cosT.d, sinT.d], writes=[rt.d])
            qkv = qk.t[:].rearrange("p a (b d) -> p (a b) d", d=64)
            kb.op(DVE, lambda: nc.vector.tensor_tensor(out=qkv[:, :, 0:8], in0=rt.t[:, 0], in1=rt.t[:, 1],
                                                       op=ALU.subtract), reads=[rt.d], writes=[qk.d])
            kb.op(DVE, lambda: nc.vector.tensor_tensor(out=qkv[:, :, 8:16], in0=rt.t[:, 2], in1=rt.t[:, 3],
                                                       op=ALU.add), reads=[rt.d], writes=[qk.d])
            vt, vs = vtmp[t % 2], vst[t % 2]
            kb.op(ACT, lambda: nc.scalar.activation(out=vt.t[:], in_=PS_X[0].t[:, 0:512], func=AF.Gelu_apprx_tanh),
                  reads=[XD[0][0]], writes=[vt.d])
            kb.op(ACT, lambda: nc.scalar.activation(
                out=vaug.t[:, t, :, 0:128], in_=PS_X[0].t[:, 512:1024].rearrange("p (h d) -> p h d", h=4),
                func=AF.Copy), reads=[XD[0][1]], writes=[vaug.d])
            for g in range(4):
                kb.op(ACT, lambda g=g: nc.scalar.activation(
                    out=junk.t[:], in_=vt.t[:, g * 128:(g + 1) * 128], func=AF.Square,
                    accum_out=vs.t[:, g:g + 1]), reads=[vt.d], writes=[junk.d, vs.d])
            kb.op(ACT, lambda: nc.scalar.activation(out=vs.t[:, 4:8], in_=vs.t[:, 0:4], func=AF.Sqrt,
                                                    bias=epsc.t[:, 0:1], scale=1.0 / 128),
                  reads=[vs.d, epsc.d], writes=[vs.d])
            kb.op(DVE, lambda: nc.vector.reciprocal(out=vs.t[:, 8:12], in_=vs.t[:, 4:8]), reads=[vs.d], writes=[vs.d])
            for g in range(4):
                kb.op(DVE, lambda g=g: nc.vector.scalar_tensor_tensor(
                    out=vg.t[:, tt, g * 128:(g + 1) * 128], in0=vt.t[:, g * 128:(g + 1) * 128],
                    scalar=vs.t[:, 8 + g:9 + g], in1=vgain.t[:, g * 128:(g + 1) * 128],
                    op0=ALU.mult, op1=ALU.mult), reads=[vt.d, vs.d, vgain.d], writes=[vg.d])

        def tqk(sbi, tt):
            t = sbi * 4 + tt
            cols = slice(tt * 128, (tt + 1) * 128)
            qk = qkb[t % 2]
            for a in range(2):
                for h in range(4):
                    i = a * 4 + h
                    kb.op(PE, lambda a=a, h=h, i=i: nc.tensor.transpose(
                        PS_T.t[:, i * 128:(i + 1) * 128], qk.t[:, a, h * 128:(h + 1) * 128], ident.t[:]),
                        reads=[qk.d, ident.d], writes=[PS_T.d], signal=(i == 7))
            kb.op(DVE, lambda: nc.vector.tensor_copy(
                out=qT.t[:, :, cols], in_=PS_T.t[:, 0:512].rearrange("p (h c) -> p h c", h=4)),
                reads=[PS_T.d], writes=[qT.d])
            kb.op(DVE, lambda: nc.vector.tensor_copy(
                out=kT.t[:, :, t * 128:(t + 1) * 128], in_=PS_T.t[:, 512:1024].rearrange("p (h c) -> p h c", h=4)),
                reads=[PS_T.d], writes=[kT.d])

        def u_grp(g):
            po = PS_O[g % 2]
            for kc in range(8):
                kb.op(PE, lambda kc=kc: nc.tensor.matmul(
                    po.t[:], wi.t[:, kc, g * 128:(g + 1) * 128], nT.t[:, kc, :], start=(kc == 0), stop=(kc == 7)),
                    reads=[nT.d, wi.d], writes=[po.d], signal=(kc == 7))
            kb.op(ACT, lambda: nc.scalar.activation(out=uT.t[:, g, :], in_=po.t[:], func=AF.Gelu_apprx_tanh),
                  reads=[po.d], writes=[uT.d])

        def sgu(tt):
            cols = slice(tt * 128, (tt + 1) * 128)
            po = PS_O[2]
            for g in range(4):
                kb.op(PE, lambda g=g: nc.tensor.matmul(
                    po.t[:, g * 128:(g + 1) * 128], vg.t[:, tt, g * 128:(g + 1) * 128], wT.t[:, g, :],
                    start=True, stop=True), reads=[vg.d, wT.d], writes=[po.d], signal=(g == 3))
            sg = sgt[tt % 2]
            kb.op(DVE, lambda: nc.vector.tensor_tensor(out=sg.t[:], in0=po.t[:], in1=bb.t[:], op=ALU.add),
                  reads=[po.d, bb.d], writes=[sg.d])
            kb.op(POOL, lambda: nc.gpsimd.tensor_tensor(
                out=mixT.t[:, 0:4, cols], in0=sg.t[:].rearrange("p (g c) -> p g c", g=4), in1=uT.t[:, :, cols],
                op=ALU.mult), reads=[sg.d, uT.d], writes=[mixT.d])

        chained = set()
        for sbi in range(NSB):
            t0 = sbi * 4

            def chain(tt):
                t = t0 + tt
                if t in chained:
                    return
                chained.add(t)
                norm_chain(nc, kb, G, src[t * 128:(t + 1) * 128, :], t % 2, gA, hdep[t])

            def ntr(tt):
                norm_T(nc, kb, G, (t0 + tt) % 2, PS_T, nT, tt * 128)

            chain(0); chain(1); ntr(0); chain(2); ntr(1); chain(3); ntr(2); ntr(3)
            b1_mm(sbi, 0); b1_evac(sbi, 0); u_grp(0); u_grp(1)
            b1_mm(sbi, 1); tqk(sbi, 0); b1_evac(sbi, 1); u_grp(2)
            b1_mm(sbi, 2); tqk(sbi, 1); b1_evac(sbi, 2); u_grp(3)
            b1_mm(sbi, 3); tqk(sbi, 2); b1_evac(sbi, 3); sgu(0); sgu(1); sgu(2)
            tqk(sbi, 3); sgu(3)
            nfull = sbi * 4
            items = [(h, c) for h in range(NH) for c in range(nfull + 4)]

            def emit_qk(idx):
                h, c = items[idx]
                X = PS_X[idx % 2]
                q0 = max(0, c - nfull) * 128
                for m in range(2):
                    kb.op(PE, lambda m=m: nc.tensor.matmul(
                        X.t[:, m * 512 + q0:(m + 1) * 512], kT.t[m * 64:(m + 1) * 64, h, c * 128:(c + 1) * 128],
                        qT.t[m * 64:(m + 1) * 64, h, q0:512], start=True, stop=True),
                        reads=[kT.d, qT.d], writes=XD[idx % 2], signal=(m == 1))

            emit_qk(0)
            for idx, (h, c) in enumerate(items):
                if idx + 1 < len(items):
                    emit_qk(idx + 1)
                X = PS_X[idx % 2]
                P = PT[idx % 3]
                pd = PTD[idx % 3]
                j = max(0, c - nfull)
                q0 = j * 128
                kb.op(ACT, lambda: nc.scalar.activation(
                    out=P.t[:, :, q0:512], in_=X.t[:].rearrange("p (m q) -> p m q", m=2)[:, :, q0:512],
                    func=AF.Exp, scale=0.125), reads=XD[idx % 2], writes=pd)
                if c >= nfull:
                    kb.op(POOL, lambda: nc.gpsimd.tensor_tensor(
                        out=P.t[:, :, q0:q0 + 128], in0=P.t[:, :, q0:q0 + 128],
                        in1=maskT.t[:].unsqueeze(1).broadcast_to([128, 2, 128]), op=ALU.mult),
                        reads=pd + [maskT.d], writes=pd)
                for qb in range(j, 4):
                    for m in range(2):
                        a = qb * 2 + m
                        bank, slot = a // 3, a % 3
                        kb.op(PE, lambda qb=qb, m=m, bank=bank, slot=slot: nc.tensor.matmul(
                            PS_O[bank].t[:, slot * 129:(slot + 1) * 129], P.t[:, m, qb * 128:(qb + 1) * 128],
                            vaug.t[:, c, h, 0:129], start=(c == 0 and slot == 0), stop=(c == nfull + qb),
                            skip_group_check=True),
                            reads=pd + [vaug.d], writes=[PS_O[bank].d], signal=(qb == 3 and m == 1))
                m = 1
                if c == nfull + 3 and m == 1:
                    for bank in range(3):
                        n = 3 if bank < 2 else 2
                        kb.op(DVE, lambda bank=bank, n=n: nc.vector.tensor_copy(
                            out=oacc.t[:, bank * 3:bank * 3 + n, :].rearrange("p a c -> p (a c)"),
                            in_=PS_O[bank].t[:, 0:n * 129]), reads=[PS_O[bank].d], writes=[oacc.d])
                    ov = oacc.t[:].rearrange("p (q m) c -> p q m c", m=2)
                    kb.op(DVE, lambda: nc.vector.reciprocal(out=ost.t[:, 0:8], in_=oacc.t[:, :, 128]),
                          reads=[oacc.d], writes=[ost.d])
                    osv = ost.t[:, 0:8].rearrange("p (q m) -> p q m", m=2)
                    kb.op(DVE, lambda: nc.vector.tensor_scalar(out=ost.t[:, 8:12], in0=osv[:, :, 1], scalar1=lams.t[:, 4:5],
                                                               scalar2=None, op0=ALU.mult),
                          reads=[ost.d, lams.d], writes=[ost.d])
                    kb.op(DVE, lambda: nc.vector.tensor_tensor(
                        out=o1.t[:], in0=ov[:, :, 0, 0:128], in1=osv[:, :, 0:1].broadcast_to([128, 4, 128]), op=ALU.mult),
                        reads=[oacc.d, ost.d], writes=[o1.d])
                    kb.op(DVE, lambda: nc.vector.tensor_tensor(
                        out=o2.t[:], in0=ov[:, :, 1, 0:128], in1=ost.t[:, 8:12].unsqueeze(2).broadcast_to([128, 4, 128]),
                        op=ALU.mult), reads=[oacc.d, ost.d], writes=[o2.d])
                    kb.op(DVE, lambda: nc.vector.tensor_tensor(out=o1.t[:], in0=o1.t[:], in1=o2.t[:], op=ALU.add),
                          reads=[o1.d, o2.d], writes=[o1.d])
                    kb.op(DVE, lambda: nc.vector.tensor_tensor(out=o2.t[:], in0=o1.t[:], in1=o1.t[:], op=ALU.mult),
                          reads=[o1.d], writes=[o2.d])
                    kb.op(DVE, lambda: nc.vector.reduce_sum(out=ost.t[:, 12:16], in_=o2.t[:], axis=AX.X),
                          reads=[o2.d], writes=[ost.d])
                    kb.op(ACT, lambda: nc.scalar.activation(out=ost.t[:, 16:20], in_=ost.t[:, 12:16], func=AF.Sqrt,
                                                            bias=epsc.t[:, 0:1], scale=1.0 / 128),
                          reads=[ost.d, epsc.d], writes=[ost.d])
                    kb.op(DVE, lambda: nc.vector.reciprocal(out=ost.t[:, 20:24], in_=ost.t[:, 16:20]),
                          reads=[ost.d], writes=[ost.d])
                    kb.op(DVE, lambda: nc.vector.tensor_tensor(
                        out=o1.t[:], in0=o1.t[:], in1=ost.t[:, 20:24].unsqueeze(2).broadcast_to([128, 4, 128]), op=ALU.mult),
                        reads=[o1.d, ost.d], writes=[o1.d])
                    kb.op(DVE, lambda h=h: nc.vector.tensor_tensor(
                        out=otok.t[:, :, h * 128:(h + 1) * 128], in0=o1.t[:],
                        in1=subg.t[:].unsqueeze(1).broadcast_to([128, 4, 128]), op=ALU.mult),
                        reads=[o1.d, subg.d], writes=[otok.d])
            for tt in range(4):
                cols = slice(tt * 128, (tt + 1) * 128)
                for h in range(4):
                    kb.op(PE, lambda h=h: nc.tensor.transpose(
                        PS_T.t[:, h * 128:(h + 1) * 128], otok.t[:, tt, h * 128:(h + 1) * 128], ident.t[:]),
                        reads=[otok.d, ident.d], writes=[PS_T.d], signal=(h == 3))
                kb.op(DVE, lambda: nc.vector.tensor_copy(
                    out=mixT.t[:, 4:8, cols], in_=PS_T.t[:, 0:512].rearrange("p (h c) -> p h c", h=4)),
                    reads=[PS_T.d], writes=[mixT.d])
            if sbi + 1 < NSB:
                chain(4); chain(5)
            for tt in range(4):
                t = t0 + tt
                cols = slice(tt * 128, (tt + 1) * 128)
                he = hE[t % 2]
                kb.dma(SP, he.t[:], src[t * 128:(t + 1) * 128, :], reads=[hdep[t]], writes=[he.d])
                for cb in range(2):
                    po = PS_O[cb]
                    for kc in range(8):
                        kb.op(PE, lambda kc=kc, cb=cb, po=po: nc.tensor.matmul(
                            po.t[:], mixT.t[:, kc, cols], wo.t[:, kc, cb * 512:(cb + 1) * 512],
                            start=(kc == 0), stop=(kc == 7)), reads=[mixT.d, wo.d], writes=[po.d], signal=(kc == 7))
                    kb.op(DVE, lambda cb=cb, po=po: nc.vector.tensor_tensor(
                        out=he.t[:, cb * 512:(cb + 1) * 512], in0=po.t[:], in1=he.t[:, cb * 512:(cb + 1) * 512], op=ALU.add),
                        reads=[po.d, he.d], writes=[he.d])
                kb.dma(SP, dst[t * 128:(t + 1) * 128, :], he.t[:], reads=[he.d], writes=[ddeps[t]], owner=he.d)
        phase_barrier(nc, kb)


def phase_barrier(nc, kb):
    engs = [kb.pe, kb.act, kb.dve, kb.pool, kb.sp]
    tiks = [Tik(E.sem, E.cnt) for E in engs if E.cnt > 0]
    tiks += [Tik(d.dsem, d.dcnt) for d in kb.owners if d.dcnt > 0]
    for E in engs:
        assert E.pending.val is None
        for t in tiks:
            kb._wait(E, t)


def emit_ffn(nc, kb, l, S, src, hdep, dst, ddeps, cdep, env, fin):
    PE, ACT, DVE, POOL, SP = kb.pe, kb.act, kb.dve, kb.pool, kb.sp
    NT, NSB = S // 128, S // 512
    ident = env["ident"]
    with contextlib.ExitStack() as st:
        kb.phase_id = getattr(kb, "phase_id", 0) + 1
        pfx = "p%d_" % kb.phase_id

        def sb(name, shape, dt):
            return T(st.enter_context(nc.sbuf_tensor(pfx + name, list(shape), dt)), pfx + name)

        def ps(name, shape, dt):
            r = T(st.enter_context(nc.psum_tensor(pfx + name, list(shape), dt)), pfx + name)
            r.d.excl = True
            return r

        wu = sb("wu", [128, 8, 2 * DFF], BF16)
        wd = sb("wd", [128, NCG, D], BF16)
        CGC = 6
        nchunk = (NCG + CGC - 1) // CGC
        wud = [Dep("wud%d_%d" % (kb.phase_id, q)) for q in range(nchunk)]
        for q in range(nchunk):
            g0, g1 = q * CGC, min(NCG, (q + 1) * CGC)
            for kc in range(8):
                for hh in range(2):
                    c0, c1 = hh * DFF + g0 * 128, hh * DFF + g1 * 128
                    kb.dma(POOL, wu.t[:, kc, c0:c1], env["w_up"][l, kc * 128:(kc + 1) * 128, c0:c1],
                           writes=[wud[q]], waw=False)
        for cg in range(NCG):
            kb.dma(POOL, wd.t[:, cg, :], env["w_down"][l, cg * 128:(cg + 1) * 128, :], writes=[wd.d], waw=False)
        nT = sb("nT", [128, 8, 512], BF16)
        actT = sb("actT", [128, NCG, 512], BF16)
        G = dict(ident=ident)
        G["hA"] = [sb("hA%d" % i, [128, D], F32) for i in range(2)]
        G["nb"] = [sb("nb%d" % i, [128, D], BF16) for i in range(2)]
        G["ss"] = [sb("ss%d" % i, [128, 4], F32) for i in range(2)]
        hE = [sb("hE%d" % i, [128, D], F32) for i in range(2)]
        fs = [sb("fs%d" % i, [128, 4], F32) for i in range(2)]
        Ag = [sb("Ag%d" % i, [128, 512], F32) for i in range(2)]
        Av = [sb("Av%d" % i, [128, 512], F32) for i in range(2)]
        tails = sb("tails", [128, 2 * NCG, 2], F32)
        Ug = sb("Ug", [128, 514], F32)
        Uv = sb("Uv", [128, 514], F32)
        cw = sb("cw", [128, 3, 2 * NCG], F32)
        cbias = sb("cbias", [128, 2 * NCG], F32)
        gF = sb("gF", [128, D], F32)
        epsc = sb("epsc", [128, 1], F32)
        G["epsc"] = epsc
        PS_G = [ps("PS_G%d" % i, [128, 512], F32) for i in range(2)]
        PS_V = [ps("PS_V%d" % i, [128, 512], F32) for i in range(2)]
        PS_D = [ps("PS_D%d" % i, [128, 512], F32) for i in range(2)]
        PS_T = [ps("PS_T%d" % i, [128, 1024], BF16) for i in range(2)]

        kb.op(DVE, lambda: nc.vector.memset(epsc.t[:], EPS), writes=[epsc.d])
        kb.op(DVE, lambda: nc.vector.memset(Ug.t[:, 0:2], 0.0), writes=[Ug.d])
        kb.op(DVE, lambda: nc.vector.memset(Uv.t[:, 0:2], 0.0), writes=[Uv.d])
        bcast_load(kb, gF, env["ffn_norm"][l], D, cdep)
        if fin:
            gfin = sb("gfin", [128, D], F32)
            bcast_load(kb, gfin, env["final_norm"], D, cdep)
        for j in range(3):
            kb.dma(SP, cw.t[:, j, :], env["conv_w"][l, j].rearrange("(g p) -> p g", p=128), writes=[cw.d],
                   waw=False, allow_slow_non_contiguous=True)
        kb.dma(SP, cbias.t[:], env["conv_b"][l].rearrange("(g p) -> p g", p=128), writes=[cbias.d],
               allow_slow_non_contiguous=True)

        def chain(t):
            norm_chain(nc, kb, G, src[t * 128:(t + 1) * 128, :], t % 2, gF, hdep[t])

        def ntr(t):
            norm_T(nc, kb, G, t % 2, PS_T[t % 2], nT, (t % 4) * 128)

        chain(0); chain(1); ntr(0); chain(2); ntr(1); chain(3); ntr(2); ntr(3)
        for sbi in range(NSB):
            t0 = sbi * 4
            for cg in range(NCG):
                pg, pv = PS_G[cg % 2], PS_V[cg % 2]
                ag, av = Ag[cg % 2], Av[cg % 2]
                for (P, c0) in ((pg, cg * 128), (pv, DFF + cg * 128)):
                    for kc in range(8):
                        kb.op(PE, lambda P=P, c0=c0, kc=kc: nc.tensor.matmul(
                            P.t[:], wu.t[:, kc, c0:c0 + 128], nT.t[:, kc, :], start=(kc == 0), stop=(kc == 7)),
                            reads=[wud[cg // CGC], nT.d], writes=[P.d], signal=(kc == 7))
                for (P, U, A, gi, EV, ev) in ((pg, Ug, ag, cg, DVE, nc.vector), (pv, Uv, av, NCG + cg, POOL, nc.gpsimd)):
                    w0, w1, w2 = (cw.t[:, j, gi:gi + 1] for j in range(3))
                    if sbi > 0:
                        kb.op(ACT, lambda U=U, gi=gi: nc.scalar.activation(out=U.t[:, 0:2], in_=tails.t[:, gi, :], func=AF.Copy),
                              reads=[tails.d], writes=[U.d])
                    kb.op(ACT, lambda U=U, P=P: nc.scalar.activation(out=U.t[:, 2:514], in_=P.t[:], func=AF.Copy),
                          reads=[P.d], writes=[U.d])
                    if sbi < NSB - 1:
                        kb.op(ACT, lambda U=U, gi=gi: nc.scalar.activation(out=tails.t[:, gi, :], in_=U.t[:, 512:514], func=AF.Copy),
                              reads=[U.d], writes=[tails.d])
                    kb.op(EV, lambda ev=ev, U=U, A=A, gi=gi, w2=w2: ev.tensor_scalar(
                        out=A.t[:], in0=U.t[:, 2:514], scalar1=w2, scalar2=cbias.t[:, gi:gi + 1], op0=ALU.mult, op1=ALU.add),
                        reads=[U.d, cw.d, cbias.d], writes=[A.d])
                    kb.op(DVE, lambda U=U, A=A, w1=w1: nc.vector.scalar_tensor_tensor(
                        out=A.t[:], in0=U.t[:, 1:513], scalar=w1, in1=A.t[:], op0=ALU.mult, op1=ALU.add),
                        reads=[U.d, A.d, cw.d], writes=[A.d])
                    kb.op(DVE, lambda U=U, A=A, w0=w0: nc.vector.scalar_tensor_tensor(
                        out=A.t[:], in0=U.t[:, 0:512], scalar=w0, in1=A.t[:], op0=ALU.mult, op1=ALU.add),
                        reads=[U.d, A.d, cw.d], writes=[A.d])
                kb.op(ACT, lambda ag=ag: nc.scalar.activation(out=ag.t[:], in_=ag.t[:], func=AF.Silu),
                      reads=[ag.d], writes=[ag.d])
                kb.op(POOL, lambda ag=ag, av=av, cg=cg: nc.gpsimd.tensor_tensor(
                    out=actT.t[:, cg, :], in0=ag.t[:], in1=av.t[:], op=ALU.mult),
                    reads=[ag.d, av.d], writes=[actT.d])
            nxt = (sbi + 1) * 4 if sbi + 1 < NSB else None
            if nxt is not None:
                chain(nxt); chain(nxt + 1)
            for tt in range(4):
                t = t0 + tt
                cols = slice(tt * 128, (tt + 1) * 128)
                he = hE[t % 2]
                if nxt is not None:
                    if tt >= 1:
                        ntr(nxt + tt - 1)
                        if tt + 1 < 4:
                            chain(nxt + tt + 1)
                kb.dma(SP, he.t[:], src[t * 128:(t + 1) * 128, :], reads=[hdep[t]], writes=[he.d])
                for cb in range(2):
                    po = PS_D[cb]
                    for cg in range(NCG):
                        kb.op(PE, lambda cg=cg, cb=cb, po=po: nc.tensor.matmul(
                            po.t[:], actT.t[:, cg, cols], wd.t[:, cg, cb * 512:(cb + 1) * 512],
                            start=(cg == 0), stop=(cg == NCG - 1)), reads=[actT.d, wd.d], writes=[po.d],
                            signal=(cg == NCG - 1))
                    kb.op(DVE, lambda cb=cb, po=po: nc.vector.tensor_tensor(
                        out=he.t[:, cb * 512:(cb + 1) * 512], in0=po.t[:], in1=he.t[:, cb * 512:(cb + 1) * 512], op=ALU.add),
                        reads=[po.d, he.d], writes=[he.d])
                if fin:
                    f = fs[t % 2]
                    for hh, jb in ((0, Ag[0]), (1, Av[0])):
                        kb.op(ACT, lambda hh=hh, jb=jb: nc.scalar.activation(
                            out=jb.t[:], in_=he.t[:, hh * 512:(hh + 1) * 512], func=AF.Square,
                            accum_out=f.t[:, 3 * hh:3 * hh + 1]), reads=[he.d], writes=[jb.d, f.d])
                    kb.op(DVE, lambda: nc.vector.tensor_tensor(out=f.t[:, 0:1], in0=f.t[:, 0:1], in1=f.t[:, 3:4], op=ALU.add),
                          reads=[f.d], writes=[f.d])
                    kb.op(ACT, lambda: nc.scalar.activation(out=f.t[:, 1:2], in_=f.t[:, 0:1], func=AF.Sqrt,
                                                            bias=epsc.t[:, 0:1], scale=1.0 / D),
                          reads=[f.d, epsc.d], writes=[f.d])
                    kb.op(DVE, lambda: nc.vector.reciprocal(out=f.t[:, 2:3], in_=f.t[:, 1:2]), reads=[f.d], writes=[f.d])
                    kb.op(DVE, lambda: nc.vector.scalar_tensor_tensor(
                        out=he.t[:], in0=he.t[:], scalar=f.t[:, 2:3], in1=gfin.t[:], op0=ALU.mult, op1=ALU.mult),
                        reads=[he.d, f.d, gfin.d], writes=[he.d])
                kb.dma(SP, dst[t * 128:(t + 1) * 128, :], he.t[:], reads=[he.d], writes=[ddeps[t]], owner=he.d)
            if nxt is not None:
                ntr(nxt + 3)
        phase_barrier(nc, kb)


_CACHE = {}


def kernel(**inputs):
    S = 4096
    nb = 8
    if S not in _CACHE:
        _CACHE[S] = build_program(S, 4)
    nc = _CACHE[S]
    names = ["attn_norm", "w_in", "sgu_v_norm", "sgu_w_spatial", "sgu_b_spatial", "lambda_q1", "lambda_k1",
             "lambda_q2", "lambda_k2", "subln_gain", "w_out", "ffn_norm", "w_up", "conv_w", "conv_b", "w_down",
             "final_norm"]
    shared = {n: np.ascontiguousarray(np.asarray(inputs[n], dtype=np.float32)) for n in names}
    x = np.asarray(inputs["x"], dtype=np.float32)
    pos = np.asarray(inputs["positions"], dtype=np.int32)
    in_maps = []
    for b in range(nb):
        m = dict(shared)
        m["x"] = np.ascontiguousarray(x[b])
        m["positions"] = np.ascontiguousarray(pos[b])
        in_maps.append(m)
    res = run_bass_kernel_spmd(nc, in_maps, core_ids=list(range(nb)))
    return np.stack([np.asarray(r["out"], dtype=np.float32) for r in res.results], axis=0)
```
